# Optimizing a Trainium2 kernel written in Bass

```python
import jax
import jax.numpy as jnp
from jax import lax
import numpy as np

D_MODEL = 1024
BATCH = 8
SEQ = 8192
DEPTH = 2

EPS = 1e-6
LRU_WIDTH = D_MODEL // 2
LRU_BLOCKS = 8
LRU_BLOCK = LRU_WIDTH // LRU_BLOCKS
LRU_CONV = 4
LRU_C = 8.0
HG_HEADS = 4
HG_DK = 128
HG_DV = 128
HG_WIDTH = HG_HEADS * HG_DV
HG_CHUNK = 64
EVEN_SPLITS = (LRU_WIDTH, 2 * LRU_WIDTH, 2 * LRU_WIDTH + HG_HEADS * HG_DK, 2 * LRU_WIDTH + 2 * HG_HEADS * HG_DK, 2 * LRU_WIDTH + 2 * HG_HEADS * HG_DK + HG_WIDTH)
EVEN_IN = 2 * LRU_WIDTH + 2 * HG_HEADS * HG_DK + 2 * HG_WIDTH
EVEN_OUT = LRU_WIDTH + HG_WIDTH
SGU_WIDTH = D_MODEL
SGU_GROUPS = 8
SGU_GROUP = SGU_WIDTH // SGU_GROUPS
SGU_CHUNK = 128
D_FF = 2816
FFN_CONV = 3
N_EVEN = (DEPTH + 1) // 2
N_ODD = DEPTH // 2

kernel_name = "hybrid_rglru_hgrn2_gmlp_convffn"


def rms_norm(x, g):
    xf = x.astype(jnp.float32)
    xf = xf * lax.rsqrt(jnp.mean(xf * xf, axis=-1, keepdims=True) + EPS)
    return (xf * g.astype(jnp.float32)).astype(x.dtype)


def layer_norm(x, g, b):
    xf = x.astype(jnp.float32)
    xc = xf - jnp.mean(xf, axis=-1, keepdims=True)
    var = jnp.mean(xc * xc, axis=-1, keepdims=True)
    return (xc * lax.rsqrt(var + EPS) * g.astype(jnp.float32) + b.astype(jnp.float32)).astype(x.dtype)


def causal_dwconv(x, w, b):
    k_w = w.shape[0]
    t = x.shape[1]
    xp = jnp.pad(x, ((0, 0), (k_w - 1, 0), (0, 0)))
    out = b
    for k in range(k_w):
        out = out + xp[:, k:k + t] * w[k]
    return out


def rg_lru(x, w_a, b_a, w_x, b_x, lam):
    bsz, t, _ = x.shape
    xf = x.astype(jnp.float32)
    xb = xf.reshape(bsz, t, LRU_BLOCKS, LRU_BLOCK)
    gate_r = jax.nn.sigmoid(jnp.einsum("btni,nij->btnj", xb, w_a.astype(jnp.float32)).reshape(bsz, t, LRU_WIDTH) + b_a.astype(jnp.float32))
    gate_i = jax.nn.sigmoid(jnp.einsum("btni,nij->btnj", xb, w_x.astype(jnp.float32)).reshape(bsz, t, LRU_WIDTH) + b_x.astype(jnp.float32))
    log_a = -LRU_C * gate_r * jax.nn.softplus(-lam.astype(jnp.float32))
    a = jnp.exp(log_a)
    u = jnp.sqrt(-jnp.expm1(2.0 * log_a)) * (gate_i * xf)

    def combine(left, right):
        a_l, h_l = left
        a_r, h_r = right
        return a_l * a_r, a_r * h_l + h_r

    _, h = lax.associative_scan(combine, (a, u), axis=1)
    return h


def hgrn2(q, f_logit, v, g, lb, g_norm):
    bsz, t = q.shape[:2]
    n_c = t // HG_CHUNK
    lbh = lb.astype(jnp.float32).reshape(HG_HEADS, HG_DK)
    f = lbh + (1.0 - lbh) * jax.nn.sigmoid(f_logit.astype(jnp.float32))
    shp_k = (bsz, n_c, HG_CHUNK, HG_HEADS, HG_DK)
    shp_v = (bsz, n_c, HG_CHUNK, HG_HEADS, HG_DV)
    k = (1.0 - f).reshape(shp_k)
    qf = jax.nn.silu(q.astype(jnp.float32)).reshape(shp_k)
    vf = v.astype(jnp.float32).reshape(shp_v)
    b_cum = jnp.cumsum(jnp.log(f).reshape(shp_k), axis=2)
    b_mid = b_cum[:, :, HG_CHUNK // 2 - 1:HG_CHUNK // 2]
    b_last = b_cum[:, :, HG_CHUNK - 1:]
    scores = jnp.einsum("bnthd,bnshd->bnhts", qf * jnp.exp(b_cum - b_mid), k * jnp.exp(b_mid - b_cum))
    causal = jnp.tril(jnp.ones((HG_CHUNK, HG_CHUNK), dtype=bool))
    scores = jnp.where(causal, scores, 0.0)
    o_intra = jnp.einsum("bnhts,bnshv->bnthv", scores, vf)
    q_in = qf * jnp.exp(b_cum)
    kv = jnp.einsum("bnshd,bnshv->bnhdv", k * jnp.exp(b_last - b_cum), vf)
    decay = jnp.exp(b_last[:, :, 0])

    def step(state, xs):
        d_n, kv_n, q_n = xs
        o_n = jnp.einsum("bthd,bhdv->bthv", q_n, state)
        return d_n[..., None] * state + kv_n, o_n

    s0 = jnp.zeros((bsz, HG_HEADS, HG_DK, HG_DV), jnp.float32)
    _, o_inter = lax.scan(step, s0, (jnp.moveaxis(decay, 1, 0), jnp.moveaxis(kv, 1, 0), jnp.moveaxis(q_in, 1, 0)))
    o = (o_intra + jnp.moveaxis(o_inter, 0, 1)).reshape(bsz, t, HG_HEADS, HG_DV)
    o = rms_norm(o, g_norm) * jax.nn.silu(g.astype(jnp.float32))
    return o.reshape(bsz, t, HG_WIDTH)


def even_mixer(h, w_in, conv_w, conv_b, ga_w, ga_b, gx_w, gx_b, lam, lb, hg_norm, w_out):
    bsz, t, _ = h.shape
    z = h @ w_in
    y_gate, x_rec, q, f_logit, v, g = jnp.split(z, list(EVEN_SPLITS), axis=-1)
    x_rec = causal_dwconv(x_rec, conv_w, conv_b)
    out_a = jax.nn.gelu(y_gate.astype(jnp.float32)) * rg_lru(x_rec, ga_w, ga_b, gx_w, gx_b, lam)
    out_b = hgrn2(q.reshape(bsz, t, HG_HEADS, HG_DK), f_logit.reshape(bsz, t, HG_HEADS, HG_DK), v.reshape(bsz, t, HG_HEADS, HG_DV), g.reshape(bsz, t, HG_HEADS, HG_DV), lb, hg_norm)
    return jnp.concatenate([out_a, out_b], axis=-1).astype(h.dtype) @ w_out


def odd_mixer(h, w_in, b_in, ln_g, ln_b, w_s, b_s, w_out):
    bsz, t, _ = h.shape
    n_c = t // SGU_CHUNK
    z = jax.nn.gelu(h @ w_in + b_in)
    u, v = jnp.split(z, 2, axis=-1)
    v = layer_norm(v, ln_g, ln_b).reshape(bsz, n_c, SGU_CHUNK, SGU_GROUPS, SGU_GROUP)
    w_causal = jnp.where(jnp.tril(jnp.ones((SGU_CHUNK, SGU_CHUNK), dtype=bool)), w_s, 0.0)
    sv = jnp.einsum("gts,bnsgc->bntgc", w_causal, v) + b_s.T[:, :, None]
    return (u * sv.reshape(bsz, t, SGU_WIDTH)) @ w_out


def conv_ffn(h, w_up, conv_w, conv_b, w_down):
    gate, up = jnp.split(h @ w_up, 2, axis=-1)
    gate = causal_dwconv(gate, conv_w, conv_b)
    return (jax.nn.silu(gate) * up) @ w_down


def setup_inputs(seed: int = 0) -> dict:
    key = jax.random.key(seed)
    ks = jax.random.split(key, 26)
    f32 = jnp.float32

    def nrm(k, shape, scale):
        return jax.random.normal(k, shape, f32) * scale

    def gain(k, shape):
        return 1.0 + 0.02 * jax.random.normal(k, shape, f32)

    a_c = jax.random.uniform(ks[11], (N_EVEN, LRU_WIDTH), f32, 0.9, 0.999)
    s = a_c ** (1.0 / LRU_C)
    lam = jnp.log(s) - jnp.log1p(-s)
    return {
        "x": nrm(ks[0], (BATCH, SEQ, D_MODEL), 1.0),
        "norm_mix": gain(ks[1], (DEPTH, D_MODEL)),
        "norm_ffn": gain(ks[2], (DEPTH, D_MODEL)),
        "norm_final": gain(ks[3], (D_MODEL,)),
        "ev_w_in": nrm(ks[4], (N_EVEN, D_MODEL, EVEN_IN), D_MODEL ** -0.5),
        "ev_conv_w": nrm(ks[5], (N_EVEN, LRU_CONV, LRU_WIDTH), LRU_CONV ** -0.5),
        "ev_conv_b": nrm(ks[6], (N_EVEN, LRU_WIDTH), 0.02),
        "ev_gate_a_w": nrm(ks[7], (N_EVEN, LRU_BLOCKS, LRU_BLOCK, LRU_BLOCK), LRU_BLOCK ** -0.5),
        "ev_gate_a_b": nrm(ks[8], (N_EVEN, LRU_WIDTH), 0.02),
        "ev_gate_x_w": nrm(ks[9], (N_EVEN, LRU_BLOCKS, LRU_BLOCK, LRU_BLOCK), LRU_BLOCK ** -0.5),
        "ev_gate_x_b": nrm(ks[10], (N_EVEN, LRU_WIDTH), 0.02),
        "ev_lru_lambda": lam,
        "hg_lb_logits": nrm(ks[12], (DEPTH + 1, HG_HEADS * HG_DK), 0.1),
        "ev_hg_norm": gain(ks[13], (N_EVEN, HG_DV)),
        "ev_w_out": nrm(ks[14], (N_EVEN, EVEN_OUT, D_MODEL), EVEN_OUT ** -0.5),
        "od_w_in": nrm(ks[15], (N_ODD, D_MODEL, 2 * SGU_WIDTH), D_MODEL ** -0.5),
        "od_b_in": nrm(ks[16], (N_ODD, 2 * SGU_WIDTH), 0.02),
        "od_ln_g": gain(ks[17], (N_ODD, SGU_WIDTH)),
        "od_ln_b": nrm(ks[18], (N_ODD, SGU_WIDTH), 0.02),
        "od_w_s": nrm(ks[19], (N_ODD, SGU_GROUPS, SGU_CHUNK, SGU_CHUNK), 0.5 * SGU_CHUNK ** -0.5),
        "od_b_s": gain(ks[20], (N_ODD, SGU_GROUPS, SGU_CHUNK)),
        "od_w_out": nrm(ks[21], (N_ODD, SGU_WIDTH, D_MODEL), SGU_WIDTH ** -0.5),
        "ffn_w_up": nrm(ks[22], (DEPTH, D_MODEL, 2 * D_FF), D_MODEL ** -0.5),
        "ffn_conv_w": nrm(ks[23], (DEPTH, FFN_CONV, D_FF), FFN_CONV ** -0.5),
        "ffn_conv_b": nrm(ks[24], (DEPTH, D_FF), 0.02),
        "ffn_w_down": nrm(ks[25], (DEPTH, D_FF, D_MODEL), D_FF ** -0.5),
    }


def reference(x, norm_mix, norm_ffn, norm_final, ev_w_in, ev_conv_w, ev_conv_b, ev_gate_a_w, ev_gate_a_b, ev_gate_x_w, ev_gate_x_b, ev_lru_lambda, hg_lb_logits, ev_hg_norm, ev_w_out, od_w_in, od_b_in, od_ln_g, od_ln_b, od_w_s, od_b_s, od_w_out, ffn_w_up, ffn_conv_w, ffn_conv_b, ffn_w_down):
    lower_bounds = jnp.cumsum(jax.nn.softmax(hg_lb_logits.astype(jnp.float32), axis=0), axis=0)
    h = x
    for layer in range(DEPTH):
        hn = rms_norm(h, norm_mix[layer])
        if layer % 2 == 0:
            e = layer // 2
            mix = even_mixer(hn, ev_w_in[e], ev_conv_w[e], ev_conv_b[e], ev_gate_a_w[e], ev_gate_a_b[e], ev_gate_x_w[e], ev_gate_x_b[e], ev_lru_lambda[e], lower_bounds[layer], ev_hg_norm[e], ev_w_out[e])
        else:
            o = layer // 2
            mix = odd_mixer(hn, od_w_in[o], od_b_in[o], od_ln_g[o], od_ln_b[o], od_w_s[o], od_b_s[o], od_w_out[o])
        h = h + mix.astype(h.dtype)
        h = h + conv_ffn(rms_norm(h, norm_ffn[layer]), ffn_w_up[layer], ffn_conv_w[layer], ffn_conv_b[layer], ffn_w_down[layer]).astype(h.dtype)
    return rms_norm(h, norm_final)
```

```python
from contextlib import ExitStack

import ml_dtypes
import numpy as np

import concourse.bass as bass
import concourse.mybir as mybir
from concourse.bass_utils import run_bass_kernel_spmd

F32 = mybir.dt.float32
BF16 = mybir.dt.bfloat16
AF = mybir.ActivationFunctionType
ALU = mybir.AluOpType
AX = mybir.AxisListType

PE, ACT, DVE, POOL, SP = "tensor", "scalar", "vector", "gpsimd", "sync"
ENGINES = (PE, ACT, DVE, POOL, SP)

D = 1024
SEQ = 8192
NB = 8
DFF = 2816
NKF = DFF // 128
TT = 512
EPS = 1e-6
NSLOT = 6


class Prog:
    def __init__(self, nc, stack):
        self.nc = nc
        self.stack = stack
        self.streams = {e: [] for e in ENGINES}
        self.sem = {}
        self.cnt = {}
        self.seen = {e: {} for e in ENGINES}
        self.lastw = {}
        self.readers = {}
        self.gen = {}
        for e in (PE, ACT, DVE, POOL):
            self._mksem(e)

    def _mksem(self, name):
        if name not in self.sem:
            self.sem[name] = self.stack.enter_context(self.nc.semaphore("s_" + name))
            self.cnt[name] = 0

    def alloc(self, base):
        self.gen[base] = self.gen.get(base, 0) + 1
        return f"{base}#{self.gen[base]}"

    def _norm(self, keys):
        out = []
        for k in keys:
            base, _, g = k.partition("#")
            if g:
                assert self.gen.get(base) == int(g), f"stale rotating buffer {k} (now gen {self.gen.get(base)})"
            out.append(base)
        return out

    def _deps(self, eng, reads, writes):
        deps = {}

        def add(tok):
            if tok is None:
                return
            s, v = tok
            if eng == PE and s == PE:
                return
            if deps.get(s, 0) < v:
                deps[s] = v

        for r in reads:
            add(self.lastw.get(r))
        for w in writes:
            add(self.lastw.get(w))
            for t in self.readers.get(w, ()):
                add(t)
        waits = []
        for s, v in deps.items():
            if self.seen[eng].get(s, 0) < v:
                self.seen[eng][s] = v
                waits.append((s, v))
        return waits

    def _commit(self, tok, reads, writes):
        for r in reads:
            self.readers.setdefault(r, []).append(tok)
        for w in writes:
            self.lastw[w] = tok
            self.readers[w] = []

    def op(self, eng, emit, reads=(), writes=()):
        reads, writes = self._norm(reads), self._norm(writes)
        waits = self._deps(eng, reads, writes)
        self.cnt[eng] += 1
        tok = (eng, self.cnt[eng])
        self.streams[eng].append((waits, emit, (eng, 1)))
        self._commit(tok, reads, writes)
        return tok

    def dma(self, queue, lane, out, in_, reads=(), writes=(), **kw):
        self._mksem(lane)
        reads, writes = self._norm(reads), self._norm(writes)
        waits = self._deps(queue, reads, writes)
        self.cnt[lane] += 16
        tok = (lane, self.cnt[lane])
        self.streams[queue].append(
            (waits, lambda e: e.dma_start(out=out, in_=in_, **kw), (lane, 16))
        )
        self._commit(tok, reads, writes)
        return tok

    def wait_all(self, eng, lanes):
        waits = [(l, self.cnt[l]) for l in lanes if self.cnt.get(l, 0) > 0]
        self.streams[eng].append((waits, None, None))

    def emit(self):
        with self.nc.Block() as block:
            for name in ENGINES:
                stream = self.streams[name]

                def body(eng, stream=stream):
                    for waits, emit, inc in stream:
                        for s, v in waits:
                            eng.wait_ge(self.sem[s], v)
                        if emit is not None:
                            ins = emit(eng)
                            ins.then_inc(self.sem[inc[0]], inc[1])

                getattr(block, name)(body)


def bcast_last(ap, m):
    return bass.AP(ap.tensor, ap.offset, [list(x) for x in ap.ap] + [[0, m]])


def bcast_mid(ap, m):
    a = [list(x) for x in ap.ap]
    return bass.AP(ap.tensor, ap.offset, [a[0], [0, m]] + a[1:])


def split_last(ap, a, b):
    l = [list(x) for x in ap.ap]
    assert l[-1][1] == a * b
    st = l[-1][0]
    return bass.AP(ap.tensor, ap.offset, l[:-1] + [[st * b, a], [st, b]])


PC = {}
_o = 0
for _n, _w in (("gmix0", 8), ("gffn0", 8), ("gmix1", 8), ("gffn1", 8), ("evcw", 16), ("evcb", 4),
               ("gab", 4), ("gxb", 4), ("lam", 4), ("hgl", 12), ("hgn", 1), ("odbu", 8), ("odg", 8),
               ("ffcw", 132), ("ffcb", 44)):
    PC[_n] = _o
    _o += _w
NPAR = _o


def fm(v, nch):
    return np.ascontiguousarray(np.asarray(v, np.float32).reshape(nch, 128).T)


def pack_params(inp):
    par = np.zeros((128, NPAR), np.float32)

    def put(name, arr):
        par[:, PC[name]:PC[name] + arr.shape[1]] = arr

    put("gmix0", fm(inp["norm_mix"][0], 8))
    put("gmix1", fm(inp["norm_mix"][1], 8))
    put("gffn0", fm(inp["norm_ffn"][0], 8))
    put("gffn1", fm(inp["norm_ffn"][1], 8))
    cw = np.asarray(inp["ev_conv_w"][0], np.float32)
    put("evcw", np.ascontiguousarray(cw.reshape(4, 4, 128).transpose(2, 1, 0).reshape(128, 16)))
    put("evcb", fm(inp["ev_conv_b"][0], 4))
    put("gab", fm(inp["ev_gate_a_b"][0], 4))
    put("gxb", fm(inp["ev_gate_x_b"][0], 4))
    put("lam", fm(inp["ev_lru_lambda"][0], 4))
    hl = np.asarray(inp["hg_lb_logits"], np.float32)
    put("hgl", np.ascontiguousarray(hl.reshape(3, 4, 128).transpose(2, 1, 0).reshape(128, 12)))
    put("hgn", np.asarray(inp["ev_hg_norm"][0], np.float32).reshape(128, 1))
    put("odbu", fm(inp["od_b_in"][0][:1024], 8))
    put("odg", fm(inp["od_ln_g"][0], 8))
    fw = np.asarray(inp["ffn_conv_w"], np.float32)
    put("ffcw", np.ascontiguousarray(fw.reshape(2, 3, NKF, 128).transpose(3, 0, 2, 1).reshape(128, 132)))
    fb = np.asarray(inp["ffn_conv_b"], np.float32)
    put("ffcb", np.ascontiguousarray(fb.reshape(2, NKF, 128).transpose(2, 0, 1).reshape(128, 44)))
    return par


def make_consts():
    s = np.arange(128)[:, None]
    t = np.arange(128)[None, :]
    tril = (s <= t).astype(np.float32)
    m2 = ((s <= t) & ((s // 64) == (t // 64))).astype(np.float32)
    m01 = np.ones((128, TT), np.float32)
    m01[:, ::64] = 0.0
    cst = np.concatenate([tril, m2, m01], axis=1)
    cstb = np.concatenate([np.eye(128), np.ones((128, 128))], axis=1).astype(ml_dtypes.bfloat16)
    return cst, cstb


def build(T=SEQ, upto=99):
    NT = T // TT
    nc = bass.Bass("TRN2", target_bir_lowering=False)

    def din(name, shape, dt=F32):
        return nc.dram_tensor(name, list(shape), dt, kind="ExternalInput").ap()

    x = din("x", [T, D])
    out = nc.dram_tensor("out", [T, D], F32, kind="ExternalOutput").ap()
    par_d = din("par", [128, NPAR])
    cst_d = din("cst", [128, 768])
    cstb_d = din("cstb", [128, 256], BF16)
    gfin_d = din("gfin", [1, D])
    odbv_d = din("odbv", [1, D])
    odlb_d = din("odlb", [1, D])
    odbs_d = din("odbs", [1, 1024])
    wsT_d = din("wsT", [128, 1024])
    ga_d = din("ga", [8, 64, 64])
    gx_d = din("gx", [8, 64, 64])
    W_evin = din("w_evin", [D, 3072])
    W_evout = din("w_evout", [D, D])
    W_odin = din("w_odin", [D, 2048])
    W_odout = din("w_odout", [D, D])
    W_up = [din("w_up0", [D, 2 * DFF]), din("w_up1", [D, 2 * DFF])]
    W_dn = [din("w_dn0", [DFF, D]), din("w_dn1", [DFF, D])]
    NPT = 48
    wscr = nc.dram_tensor("wscr", [NPT, 128, 4096], BF16, kind="Internal").ap()

    with ExitStack() as st:
        def sb(name, shape, dt=F32):
            return st.enter_context(nc.sbuf_tensor("sb_" + name, list(shape), dt))

        def ps(name, shape, dt=F32):
            return st.enter_context(nc.psum_tensor("ps_" + name, list(shape), dt))

        p = Prog(nc, st)

        H = sb("H", [128, 4, D])
        xn_tm = [sb(f"xntm{i}", [128, D], BF16) for i in range(2)]
        xnT = sb("xnT", [128, 8, TT], BF16)
        mixo = sb("mixo", [128, 8, TT], BF16)
        actb = sb("actb", [128, NKF, TT], BF16)
        ring = [sb(f"ring{i}", [128, 8, 512], BF16) for i in range(NSLOT)]
        NT32, NT16 = 14, 16
        t32 = [sb(f"t32_{i}", [128, 516]) for i in range(NT32)]
        t16 = [sb(f"t16_{i}", [128, 512], BF16) for i in range(NT16)]
        ygb = sb("ygb", [128, 4, TT], BF16)
        qfb = sb("qfb", [128, 4, TT], BF16)
        gsb = sb("gsb", [128, 4, TT], BF16)
        vtm = sb("vtm", [128, 4, 512], BF16)
        kdTlo = sb("kdTlo", [128, 4, 128], BF16)
        kdThi = sb("kdThi", [128, 4, 128], BF16)
        Sch = sb("Sch", [128, 9, 128])
        Sb16 = sb("Sb16", [128, 8, 128], BF16)
        Scar = sb("Scar", [128, 4, 128])
        hcar = sb("hcar", [128, 4])
        xtail = sb("xtail", [128, 4, 3])
        gtail = sb("gtail", [128, 2, NKF, 2])
        cst = sb("cst", [128, 768])
        cstb = sb("cstb", [128, 256], BF16)
        par = sb("par", [128, NPAR])
        gfin = sb("gfin", [128, D])
        odbv = sb("odbv", [128, D])
        Qg = sb("Qg", [128, 8, 128])
        WcT = sb("WcT", [128, 8, 128], BF16)
        WcT32 = sb("WcT32", [128, 8, 128])
        BD = sb("BD", [128, 8, 128], BF16)
        dv = sb("dv", [128, 32])
        ost = sb("ost", [128, 32])
        junk = sb("junk", [128, D], BF16)
        junk2 = sb("junk2", [128, D], BF16)
        NSM = 24
        sm = [sb(f"sm{i}", [128, 8]) for i in range(NSM)]

        mmb = [ps(f"mm{i}", [128, 512]) for i in range(4)]
        trb = [ps(f"tr{i}", [128, 8, 128], BF16) for i in range(2)]
        msb = [ps(f"ms{i}", [128, 512]) for i in range(2)]

        eps_ap = dv[:, 20:21]
        one_ap = dv[:, 21:22]
        tril = cst[:, 0:128]
        m2 = cst[:, 128:256]
        m01 = cst[:, 256:768]
        ident = cstb[:, 0:128]
        ones_b = cstb[:, 128:256]

        rot = {"t32": 0, "t16": 0, "mm": 0, "ms": 0, "sm": 0, "tr": 0}

        pinned = set()

        def T32(pin=False):
            for _ in range(NT32):
                i = rot["t32"] % NT32
                rot["t32"] += 1
                if i not in pinned:
                    break
            else:
                raise AssertionError("all t32 temps pinned")
            if pin:
                pinned.add(i)
            return t32[i], p.alloc(f"t32_{i}")

        def unpin(key):
            base = key.split("#")[0]
            (pinned if base.startswith("t32") else pinned16).discard(int(base.split("_")[1]))

        pinned16 = set()

        def T16(pin=False):
            for _ in range(NT16):
                i = rot["t16"] % NT16
                rot["t16"] += 1
                if i not in pinned16:
                    break
            else:
                raise AssertionError("all t16 temps pinned")
            if pin:
                pinned16.add(i)
            return t16[i], p.alloc(f"t16_{i}")

        def MM():
            i = rot["mm"] % 4
            rot["mm"] += 1
            return mmb[i], p.alloc(f"mm{i}")

        def MMX():
            i = rot.setdefault("mmx", 0) % 6
            rot["mmx"] += 1
            return (mmb[i], p.alloc(f"mm{i}")) if i < 4 else (msb[i - 4], p.alloc(f"ms{i - 4}"))

        def MS():
            i = rot["ms"] % 2
            rot["ms"] += 1
            return msb[i], p.alloc(f"ms{i}")

        def SM():
            i = rot["sm"] % NSM
            rot["sm"] += 1
            return sm[i], p.alloc(f"sm{i}")

        def TR():
            i = rot["tr"] % 2
            rot["tr"] += 1
            return trb[i], p.alloc(f"tr{i}")

        def pc(name, i=0, n=1):
            o = PC[name] + i
            return par[:, o:o + n]

        def act(out_, in_, func, reads, writes, eng=ACT, **kw):
            p.op(eng, lambda e: e.activation(out=out_, in_=in_, func=func, **kw), reads, writes)

        def tt(out_, in0, in1, op, reads, writes, eng=DVE):
            p.op(eng, lambda e: e.tensor_tensor(out=out_, in0=in0, in1=in1, op=op), reads, writes)

        def ts(out_, in0, s1, s2, op0, op1, reads, writes, eng=DVE):
            if s2 is None:
                p.op(eng, lambda e: e.tensor_scalar(out=out_, in0=in0, scalar1=s1, scalar2=None, op0=op0),
                     reads, writes)
            else:
                p.op(eng, lambda e: e.tensor_scalar(out=out_, in0=in0, scalar1=s1, scalar2=s2, op0=op0, op1=op1),
                     reads, writes)

        def stt(out_, in0, scalar, in1, op0, op1, reads, writes):
            p.op(DVE, lambda e: e.scalar_tensor_tensor(out=out_, in0=in0, scalar=scalar, in1=in1, op0=op0, op1=op1),
                 reads, writes)

        def cp(out_, in_, reads, writes, eng=ACT):
            if eng == ACT:
                p.op(ACT, lambda e: e.copy(out=out_, in_=in_), reads, writes)
            else:
                p.op(eng, lambda e: e.tensor_copy(out=out_, in_=in_), reads, writes)

        def mm(out_, pairs, reads, writes):
            def emit(e):
                n = len(pairs)
                ins = None
                for i, (l, r) in enumerate(pairs):
                    ins = e.matmul(out_, l, r, start=(i == 0), stop=(i == n - 1))
                return ins
            p.op(PE, emit, reads, writes)

        def mm_multi(items, reads, writes):
            def emit(e):
                ins = None
                for (o, l, r, s0, s1) in items:
                    ins = e.matmul(o, l, r, start=s0, stop=s1)
                return ins
            p.op(PE, emit, reads, writes)

        def run_rr(gens):
            gens = list(gens)
            while gens:
                for g in list(gens):
                    try:
                        next(g)
                    except StopIteration:
                        gens.remove(g)

        XNT = [f"xnT.{j}" for j in range(4)]
        MIXO = [f"mixo.{k}.{j}" for k in range(8) for j in range(4)]

        seq = []

        def pieces(W, ncols_list, kchunks=8, k0=0):
            return [(W, k0, kchunks, n0) for n0 in ncols_list]

        per_tile = []
        per_tile += pieces(W_evin, [512, 1536, 0, 2048, 1024, 2560])
        per_tile += pieces(W_evout, [0, 512])
        per_tile += pieces(W_up[0], [i * 512 for i in range(11)])
        for n in range(2):
            per_tile += [(W_dn[0], 0, 8, n * 512), (W_dn[0], 8, 8, n * 512), (W_dn[0], 16, 6, n * 512)]
        per_tile += pieces(W_odin, [0, 512, 1024, 1536])
        per_tile += pieces(W_odout, [0, 512])
        per_tile += pieces(W_up[1], [i * 512 for i in range(11)])
        for n in range(2):
            per_tile += [(W_dn[1], 0, 8, n * 512), (W_dn[1], 8, 8, n * 512), (W_dn[1], 16, 6, n * 512)]
        assert len(per_tile) == NPT
        seq = per_tile * NT
        wst = {"next": 0, "issued": 0, "done": 0}

        def w_issue():
            while wst["issued"] < min(len(seq), wst["done"] + NSLOT):
                i = wst["issued"]
                W, k0, nk, n0 = seq[i]
                s = i % NSLOT
                if i < NPT:
                    src = W[k0 * 128:(k0 + nk) * 128, n0:n0 + 512].rearrange("(k p) n -> p k n", p=128)
                    p.dma(POOL, f"w{s}", ring[s][:, 0:nk, :], src, writes=[f"ring{s}"])
                else:
                    p.dma(POOL, f"w{s}", ring[s][:, 0:nk, :].rearrange("p k n -> p (k n)"),
                          wscr[i % NPT][:, 0:nk * 512], reads=[f"scr{i % NPT}"], writes=[f"ring{s}"])
                wst["issued"] += 1

        def w_acquire(n=1):
            idx = list(range(wst["next"], wst["next"] + n))
            wst["next"] += n
            assert idx[-1] < wst["issued"], "weight piece not issued"
            for i in idx:
                if i < NPT and NT > 1:
                    nk = seq[i][2]
                    s = i % NSLOT
                    p.dma(SP, f"ws{s}", wscr[i][:, 0:nk * 512], ring[s][:, 0:nk, :].rearrange("p k n -> p (k n)"),
                          reads=[f"ring{s}"], writes=[f"scr{i}"])
            return [i % NSLOT for i in idx]

        def w_release(n=1):
            wst["done"] += n
            w_issue()

        p.dma(SP, "c0", par[:], par_d, writes=["par"])
        p.dma(SP, "c1", cst[:], cst_d, writes=["cst"])
        p.dma(SP, "c2", cstb[:], cstb_d, writes=["cstb"])
        p.dma(SP, "c3", gfin[:], bass.AP(gfin_d.tensor, 0, [[0, 128], [1, D]]), writes=["gfin"])
        p.dma(SP, "c4", odbv[:], bass.AP(odbv_d.tensor, 0, [[0, 128], [1, D]]), writes=["odbv"])
        p.dma(SP, "c5", Qg[:].rearrange("p g t -> p (g t)"), bass.AP(odbs_d.tensor, 0, [[0, 128], [1, 1024]]),
              writes=["Qg"])
        p.dma(SP, "c6", WcT32[:].rearrange("p g t -> p (g t)"), wsT_d, writes=["WcT32"])
        lbA, kA = T32()
        lbB, kB = T32()
        p.dma(SP, "c7", lbA[:, 0:512], bass.AP(odlb_d.tensor, 0, [[0, 128], [1, 512]]), writes=[kA])
        p.dma(SP, "c8", lbB[:, 0:512], bass.AP(odlb_d.tensor, 512, [[0, 128], [1, 512]]), writes=[kB])
        p.op(DVE, lambda e: e.memset(Scar[:], 0.0), writes=[f"Scar.{h}" for h in range(4)])
        p.op(DVE, lambda e: e.memset(hcar[:], 0.0), writes=[f"hcar.{c}" for c in range(4)])
        p.op(DVE, lambda e: e.memset(xtail[:], 0.0), writes=[f"xtail.{c}" for c in range(4)])
        p.op(DVE, lambda e: e.memset(gtail[:], 0.0), writes=[f"gtail.{l}.{j}" for l in range(2) for j in range(NKF)])
        p.op(DVE, lambda e: e.memset(BD[:], 0.0), writes=["BD"])
        p.op(DVE, lambda e: e.memset(kdTlo[:], 0.0), writes=["kdT"])
        p.op(DVE, lambda e: e.memset(kdThi[:], 0.0), writes=["kdT"])
        p.op(DVE, lambda e: e.memset(dv[:, 20:21], EPS), writes=["dv"])
        p.op(DVE, lambda e: e.memset(dv[:, 21:22], 1.0), writes=["dv"])
        for c in range(4):
            for gi, gd in enumerate((ga_d, gx_d)):
                for hh in range(2):
                    p.dma(POOL, f"bd{gi}{hh}", BD[hh * 64:(hh + 1) * 64, gi * 4 + c, hh * 64:(hh + 1) * 64],
                          gd[2 * c + hh], reads=[], writes=["BD"])
        w_issue()
        tt(WcT32[:], WcT32[:], bcast_mid(tril, 8), ALU.mult, ["WcT32", "cst"], ["WcT32"])
        cp(WcT[:], WcT32[:], ["WcT32"], ["WcT"], eng=DVE)
        for g in range(8):
            bank, kb = MS()
            lbt = (lbA if g < 4 else lbB)[:, (g % 4) * 128:(g % 4 + 1) * 128]
            kl = kA if g < 4 else kB
            mm(bank[:, 0:128], [(lbt, WcT32[:, g, :])], [kl, "WcT32"], [kb])
            tt(Qg[:, g, :], Qg[:, g, :], bank[:, 0:128], ALU.add, ["Qg", kb], ["Qg"])
        e12, ke = SM()
        e12b = sb("e12b", [128, 12])
        act(e12b[:], pc("hgl", 0, 12), AF.Exp, ["par"], ["e12b"])
        p.op(DVE, lambda e: e.tensor_reduce(out=e12[:, 0:4], in_=split_last(e12b[:], 4, 3), axis=AX.X, op=ALU.add),
             ["e12b"], [ke])
        p.op(DVE, lambda e: e.reciprocal(out=e12[:, 4:8], in_=e12[:, 0:4]), [ke], [ke])
        tt(dv[:, 0:4], e12b[:, 0:12:3], e12[:, 4:8], ALU.mult, ["e12b", ke], ["dv"])
        ts(dv[:, 4:8], dv[:, 0:4], -1.0, 1.0, ALU.mult, ALU.add, ["dv"], ["dv"])
        ts(dv[:, 8:12], dv[:, 0:4], -1.0, None, ALU.add, None, ["dv"], ["dv"])
        s1, k1 = SM()
        act(s1[:, 0:4], pc("lam", 0, 4), AF.Exp, ["par"], [k1], scale=-1.0)
        act(s1[:, 4:8], s1[:, 0:4], AF.Ln, [k1, "dv"], [k1], bias=one_ap)
        ts(dv[:, 12:16], s1[:, 4:8], -8.0, None, ALU.mult, None, [k1], ["dv"])
        ts(dv[:, 16:20], s1[:, 4:8], -16.0, None, ALU.mult, None, [k1], ["dv"])

        stage = []
        for buf, nm in ((ygb, "ygb"), (qfb, "qfb"), (gsb, "gsb"), (vtm, "vtm")):
            stage.append((buf.bitcast(F32)[:].rearrange("p a b -> p (a b)"), [f"{nm}.{c}" for c in range(4)]))

        def prefetch_x(ti):
            r0 = ti * TT
            for j in range(4):
                p.dma(SP, f"xl{j}", stage[j][0], x[r0 + j * 128:r0 + (j + 1) * 128, :], writes=stage[j][1])

        def stage_to_H():
            for j in range(4):
                p.dma(SP, f"xc{j}", H[:, j, :], stage[j][0], reads=stage[j][1], writes=[f"H.{j}"])

        def norm_subs(gname, srcs=None):
            if srcs is None:
                srcs = [(H[:, j, :], [f"H.{j}"]) for j in range(4)]
            ss, ks = SM()
            gb = bcast_last(pc(gname, 0, 8), 128)

            def sub(j):
                src, sk = srcs[j]
                xt, kx = xn_tm[j % 2], p.alloc(f"xntm{j % 2}")
                act(junk[:], src, AF.Square, sk, [ks], accum_out=ss[:, j:j + 1])
                yield
                act(ss[:, 4 + j:5 + j], ss[:, j:j + 1], AF.Ln, [ks, "dv"], [ks], scale=1.0 / D, bias=eps_ap)
                act(ss[:, 4 + j:5 + j], ss[:, 4 + j:5 + j], AF.Exp, [ks], [ks], scale=-0.5)
                yield
                if j % 2 == 0:
                    ts(xt[:], src, ss[:, 4 + j:5 + j], None, ALU.mult, None, sk + [ks], [kx])
                else:
                    act(xt[:], src, AF.Copy, sk + [ks], [kx], scale=ss[:, 4 + j:5 + j])
                yield
                tr, kt = TR()

                def emit(e):
                    ins = None
                    for k in range(8):
                        ins = e.transpose(tr[:, k, :], xt[:, k * 128:(k + 1) * 128], ident)
                    return ins
                p.op(PE, emit, [kx, "cstb"], [kt])
                tt(xnT[:, :, j * 128:(j + 1) * 128], tr[:], gb, ALU.mult, [kt, "par"], [f"xnT.{j}"])
            return [sub(j) for j in range(4)]

        def norm_gen(gname, srcs=None):
            for g in norm_subs(gname, srcs):
                for _ in g:
                    yield
                yield

        def steps(g, n):
            for _ in range(n):
                next(g, None)

        def rmsnorm_T(gname):
            run_rr([norm_gen(gname)])

        def fm_mm(slot, nk, cc):
            bank, kb = MMX()
            mm(bank[:], [(ring[slot][:, k, cc * 128:(cc + 1) * 128], xnT[:, k, :]) for k in range(nk)],
               [f"ring{slot}"] + XNT, [kb])
            return bank, kb

        def proj_gen(src, srckeys, nk_total, ride=None):
            npc = (nk_total + 7) // 8
            for n in range(2):
                banks = [MM() for _ in range(4)]
                for pi in range(npc):
                    (slot,) = w_acquire(1)
                    ks = list(range(pi * 8, min(nk_total, pi * 8 + 8)))
                    for j in range(4):
                        bank, kb = banks[j]
                        mm_multi([(bank[:], src[:, k, j * 128:(j + 1) * 128], ring[slot][:, k % 8, :],
                                   k == 0, k == nk_total - 1) for k in ks],
                                 [f"ring{slot}"] + srckeys, [kb])
                        if pi == npc - 1:
                            tt(H[:, j, n * 512:(n + 1) * 512], bank[:], H[:, j, n * 512:(n + 1) * 512], ALU.add,
                               [kb, f"H.{j}"], [f"H.{j}"])
                            if ride is not None and n == 1:
                                if j >= 2:
                                    steps(ride[j - 2], 2)
                                steps(ride[j], 3)
                        yield
                    w_release(1)
            if ride is not None:
                steps(ride[2], 2)
                steps(ride[3], 2)
                assert all(next(g, "end") == "end" for g in ride)

        def proj_blocks_gen():
            wsl = w_acquire(2)
            for j in range(4):
                for n in range(2):
                    bank, kb = MM()
                    mm(bank[:], [(mixo[:, k, j * 128:(j + 1) * 128], ring[wsl[n]][:, k, :]) for k in range(8)],
                       [f"ring{wsl[n]}"] + [f"mixo.{k}.{j}" for k in range(8)], [kb])
                    tt(H[:, j, n * 512:(n + 1) * 512], bank[:], H[:, j, n * 512:(n + 1) * 512], ALU.add,
                       [kb, f"H.{j}"], [f"H.{j}"])
                    yield
            w_release(2)

        def proj_then_norm():
            pg, ns = proj_blocks_gen(), norm_subs("gffn0")
            for j in range(4):
                steps(pg, 2)
                steps(ns[j], 3)
                if j >= 1:
                    steps(ns[j - 1], 2)
            steps(pg, 1)
            steps(ns[3], 2)
            assert all(next(g, "end") == "end" for g in ns + [pg])

        def proj_residual(src, srckeys, nk_total, side=(), ride=None):
            run_rr([proj_gen(src, srckeys, nk_total, ride)] + list(side))

        def ffn(l, side=(), ride=None):
            AK = [f"act.{k}" for k in range(NKF)]
            pend = None
            for i in range(11):
                (slot,) = w_acquire(1)
                for cc in range(4):
                    cg = i * 4 + cc
                    bank, kb = fm_mm(slot, 8, cc)
                    if cg < NKF:
                        j = cg
                        ktl = f"gtail.{l}.{j}"
                        a0, ka = T32()
                        wo = PC["ffcw"] + l * 66 + j * 3
                        bo = PC["ffcb"] + l * NKF + j
                        w0, w1, w2 = par[:, wo:wo + 1], par[:, wo + 1:wo + 2], par[:, wo + 2:wo + 3]
                        bb = par[:, bo:bo + 1]
                        act(a0[:, 0:2], gtail[:, l, j, :], AF.Identity, [ktl, "par"], [ka], scale=w0, bias=bb)
                        act(a0[:, 2:512], bank[:, 0:510], AF.Identity, [kb, "par"], [ka], scale=w0, bias=bb)
                        stt(a0[:, 0:1], gtail[:, l, j, 1:2], w1, a0[:, 0:1], ALU.mult, ALU.add, [ktl, ka, "par"], [ka])
                        stt(a0[:, 1:512], bank[:, 0:511], w1, a0[:, 1:512], ALU.mult, ALU.add, [kb, ka, "par"], [ka])
                        stt(a0[:, 0:512], bank[:, 0:512], w2, a0[:, 0:512], ALU.mult, ALU.add, [kb, ka, "par"], [ka])
                        cp(gtail[:, l, j, :], bank[:, 510:512], [kb], [ktl], eng=DVE)
                        if pend is not None:
                            pend()
                        pend = (lambda a0=a0, ka=ka, j=j:
                                act(actb[:, j, :], a0[:, 0:512], AF.Silu, [ka], [f"act.{j}"]))
                    else:
                        if pend is not None:
                            pend()
                            pend = None
                        j = cg - NKF
                        tt(actb[:, j, :], bank[:], actb[:, j, :], ALU.mult, [kb, f"act.{j}"], [f"act.{j}"])
                w_release(1)
            proj_residual(actb, AK, NKF, side, ride)

        actf = actb.bitcast(F32)

        def actv(k0):
            return actf[:, k0:k0 + 2, :].rearrange("p a b -> p (a b)"), [f"act.{k0}", f"act.{k0 + 1}"]

        def even_mixer():
            s_x, s_f = w_acquire(2)
            xcs, sgs = [], []
            pcast = None
            for c in range(4):
                bank, kb = fm_mm(s_x, 8, c)
                ktl = f"xtail.{c}"
                xc, kc = actv(2 * c)
                wo = PC["evcw"] + c * 4
                w = [par[:, wo + k:wo + k + 1] for k in range(4)]
                act(xc[:, 0:3], xtail[:, c, :], AF.Identity, [ktl, "par"], kc, scale=w[0], bias=pc("evcb", c))
                act(xc[:, 3:512], bank[:, 0:509], AF.Identity, [kb, "par"], kc, scale=w[0], bias=pc("evcb", c))
                for k in range(1, 4):
                    if k < 3:
                        stt(xc[:, 0:3 - k], xtail[:, c, k:3], w[k], xc[:, 0:3 - k], ALU.mult, ALU.add,
                            [ktl, "par"] + kc, kc)
                    stt(xc[:, 3 - k:512], bank[:, 0:509 + k], w[k], xc[:, 3 - k:512], ALU.mult, ALU.add,
                        [kb, "par"] + kc, kc)
                cp(xtail[:, c, :], bank[:, 509:512], [kb], [ktl], eng=DVE)
                hd = c
                bank, kb = fm_mm(s_f, 8, hd)
                sg, ksg = actv(8 + 2 * hd)
                act(sg[:], bank[:], AF.Sigmoid, [kb], ksg)
                sgs.append((sg, ksg))
                if pcast is not None:
                    pcast()

                def pcast(xc=xc, kc=kc):
                    xcb, kcb = T16(pin=True)
                    cp(xcb[:], xc[:, 0:512], kc, [kcb])
                    xcs.append((xc, kc, xcb, kcb))
            pcast()
            w_release(2)
            (s_y,) = w_acquire(1)
            for c in range(4):
                bank, kb = fm_mm(s_y, 8, c)
                act(ygb[:, c, :], bank[:], AF.Gelu_apprx_tanh, [kb], [f"ygb.{c}"])
            w_release(1)

            def rest_proj():
                (s_v,) = w_acquire(1)
                for j in range(4):
                    bank, kb = MM()
                    mm(bank[:], [(xnT[:, k, j * 128:(j + 1) * 128], ring[s_v][:, k, :]) for k in range(8)],
                       [f"ring{s_v}"] + XNT, [kb])
                    cp(vtm[:, j, :], bank[:], [kb], [f"vtm.{j}"])
                    yield
                w_release(1)
                (s_q,) = w_acquire(1)
                for hd in range(4):
                    bank, kb = fm_mm(s_q, 8, hd)
                    act(qfb[:, hd, :], bank[:], AF.Silu, [kb], [f"qfb.{hd}"])
                    yield
                w_release(1)
                (s_g,) = w_acquire(1)
                for hd in range(4):
                    bank, kb = fm_mm(s_g, 8, hd)
                    act(gsb[:, hd, :], bank[:], AF.Silu, [kb], [f"gsb.{hd}"])
                    yield
                w_release(1)

            def a_chain(c):
                xc, kc, xcb, kcb = xcs[c]
                br, kbr = MM()
                mm(br[:], [(BD[:, c, :], xcb[:])], ["BD", kcb], [kbr])
                r, kr = T32(pin=True)
                act(r[:, 0:512], br[:], AF.Sigmoid, [kbr, "par"], [kr], bias=pc("gab", c))
                yield
                bi, kbi = MM()
                mm(bi[:], [(BD[:, 4 + c, :], xcb[:])], ["BD", kcb], [kbi])
                unpin(kcb)
                gi, kgi = T32(pin=True)
                act(gi[:, 0:512], bi[:], AF.Sigmoid, [kbi, "par"], [kgi], bias=pc("gxb", c))
                yield
                a, ka = T32(pin=True)
                act(a[:, 0:512], r[:, 0:512], AF.Exp, [kr, "dv"], [ka], scale=dv[:, 12 + c:13 + c])
                yield
                tt(gi[:, 0:512], gi[:, 0:512], xc[:, 0:512], ALU.mult, [kgi] + kc, [kgi])
                yield
                act(r[:, 0:512], r[:, 0:512], AF.Exp, [kr, "dv"], [kr], scale=dv[:, 16 + c:17 + c])
                yield
                act(r[:, 0:512], r[:, 0:512], AF.Ln, [kr, "dv"], [kr], scale=-1.0, bias=one_ap)
                yield
                act(r[:, 0:512], r[:, 0:512], AF.Exp, [kr], [kr], scale=0.5)
                yield
                tt(gi[:, 0:512], gi[:, 0:512], r[:, 0:512], ALU.mult, [kgi, kr], [kgi])
                yield
                p.op(DVE, lambda e: e.tensor_tensor_scan(
                    out=r[:, 0:512], data0=a[:, 0:512], data1=gi[:, 0:512], initial=hcar[:, c:c + 1],
                    op0=ALU.mult, op1=ALU.add), [ka, kgi, f"hcar.{c}"], [kr])
                cp(hcar[:, c:c + 1], r[:, 511:512], [kr], [f"hcar.{c}"], eng=DVE)
                yield
                tt(mixo[:, c, :], r[:, 0:512], ygb[:, c, :], ALU.mult, [kr, f"ygb.{c}"], [f"mixo.{c}.{j}" for j in range(4)])
                unpin(kr)
                unpin(kgi)
                unpin(ka)

            def b_chain(hd):
                sg, ksg = sgs[hd]
                lb, oml, noml = dv[:, hd:hd + 1], dv[:, 4 + hd:5 + hd], dv[:, 8 + hd:9 + hd]
                if hd < 2:
                    lf, klf_l = actv(17 + 2 * hd)
                    klf = None
                else:
                    lf, klf = T32(pin=True)
                    klf_l = [klf]
                act(lf[:, 0:512], sg[:], AF.Ln, ksg + ["dv"], klf_l, scale=oml, bias=lb)
                yield
                act(sg[:], sg[:], AF.Identity, ksg + ["dv"], ksg, scale=noml, bias=oml)
                yield
                bc_, kbc = T32(pin=True)
                p.op(DVE, lambda e: e.tensor_tensor_scan(
                    out=bc_[:, 0:512], data0=m01, data1=lf[:, 0:512], initial=0.0, op0=ALU.mult, op1=ALU.add),
                    klf_l + ["cst"], [kbc])
                yield
                bmid = bc_[:, 31:512:64]
                blast = bc_[:, 63:512:64]
                tt(split_last(lf[:, 0:512], 8, 64), split_last(bc_[:, 0:512], 8, 64), bcast_last(bmid, 64),
                   ALU.subtract, [kbc], klf_l)
                sx, ksx = SM()
                sy, ksy = SM()
                sz, ksz = SM()
                yield
                act(sx[:, 0:8], bmid, AF.Exp, [kbc], [ksx])
                tt(sy[:, 0:8], blast, bmid, ALU.subtract, [kbc], [ksy])
                yield
                act(sy[:, 0:8], sy[:, 0:8], AF.Exp, [ksy], [ksy])
                act(sz[:, 0:8], blast, AF.Exp, [kbc], [ksz])
                yield
                act(bc_[:, 0:512], lf[:, 0:512], AF.Exp, klf_l, [kbc])
                yield
                act(lf[:, 0:512], lf[:, 0:512], AF.Exp, klf_l, klf_l, scale=-1.0)
                yield
                qd, kqd = T16(pin=True)
                tt(qd[:], bc_[:, 0:512], qfb[:, hd, :], ALU.mult, [kbc, f"qfb.{hd}"], [kqd])
                unpin(kbc)
                yield
                kd, kkd = T16(pin=True)
                tt(kd[:], sg[:], lf[:, 0:512], ALU.mult, ksg + klf_l, [kkd])
                if klf is not None:
                    unpin(klf)
                yield
                qi, kqi = T16(pin=True)
                tt(split_last(qi[:], 8, 64), split_last(qd[:], 8, 64), bcast_last(sx[:, 0:8], 64), ALU.mult,
                   [kqd, ksx], [kqi], eng=POOL)
                yield
                kc_, kkc = T16(pin=True)
                tt(split_last(kc_[:], 8, 64), split_last(kd[:], 8, 64), bcast_last(sy[:, 0:8], 64), ALU.mult,
                   [kkd, ksy], [kkc])
                yield
                sc, ksc = MS()
                mm_multi([(sc[:, j * 128:(j + 1) * 128], kd[:, j * 128:(j + 1) * 128], qd[:, j * 128:(j + 1) * 128],
                           True, True) for j in range(4)], [kkd, kqd], [ksc])
                unpin(kkd)
                unpin(kqd)
                ms_, kms = T16(pin=True)
                tt(split_last(ms_[:], 4, 128), split_last(sc[:], 4, 128), bcast_mid(m2, 4), ALU.mult,
                   [ksc, "cst"], [kms])
                yield
                yield "pre"
                kdTlo_, kdThi_, Sch_, Sb16_, kKd, kSch, kSb = bsets[hd % 2]
                tr, kt = TR()

                def emit(e, tr=tr, kc_=kc_):
                    ins = None
                    for j in range(4):
                        ins = e.transpose(tr[:, j, :], kc_[:, j * 128:(j + 1) * 128], ident)
                    return ins
                p.op(PE, emit, [kkc, "cstb"], [kt])
                unpin(kkc)
                cp(kdTlo_[0:64], tr[0:64, 0:4, :], [kt], kKd)
                cp(kdThi_[64:128], tr[64:128, 0:4, :], [kt], kKd)
                cp(Sch_[:, 0, :], Scar[:, hd, :], [f"Scar.{hd}"], kSch)
                yield
                kvs = []
                for half_t in range(2):
                    kv, kkv = MS()
                    items = []
                    for cq in range(4):
                        cidx = half_t * 4 + cq
                        j, hh = cidx // 2, cidx % 2
                        items.append((kv[:, cq * 128:(cq + 1) * 128], (kdThi_ if hh else kdTlo_)[:, j, :],
                                      vtm[:, j, hd * 128:(hd + 1) * 128], True, True))
                    mm_multi(items, kKd + [f"vtm.{j}" for j in range(4)], [kkv])
                    kvs.append((kv, kkv))
                for cidx in range(8):
                    kv, kkv = kvs[cidx // 4]
                    cq = cidx % 4
                    stt(Sch_[:, cidx + 1, :], Sch_[:, cidx, :], sz[:, cidx:cidx + 1], kv[:, cq * 128:(cq + 1) * 128],
                        ALU.mult, ALU.add, kSch + [ksz, kkv], kSch)
                cp(Sb16_[:], Sch_[:, 0:8, :], kSch, kSb)
                cp(Scar[:, hd, :], Sch_[:, 8, :], kSch, [f"Scar.{hd}"])
                yield
                ob, kob = MS()
                items = []
                for j in range(4):
                    items.append((ob[:, j * 128:(j + 1) * 128], vtm[:, j, hd * 128:(hd + 1) * 128], ms_[:, j * 128:(j + 1) * 128],
                                  True, False))
                    for hh in range(2):
                        cidx = 2 * j + hh
                        items.append((ob[:, cidx * 64:(cidx + 1) * 64], Sb16_[:, cidx, :], qi[:, cidx * 64:(cidx + 1) * 64],
                                      False, hh == 1))
                mm_multi(items, [kms, kqi] + kSb + [f"vtm.{j}" for j in range(4)], [kob])
                unpin(kms)
                unpin(kqi)
                sq, ksq = T16(pin=True)
                act(sq[:], ob[:], AF.Square, [kob], [ksq])
                t1, kt1 = T32(pin=True)
                act(t1[:, 0:512], ob[:], AF.Copy, [kob, "par"], [kt1], scale=pc("hgn"))
                yield "post"
                sb_, ksb = MM()
                mm(sb_[:], [(ones_b, sq[:])], ["cstb", ksq], [ksb])
                unpin(ksq)
                rs, krs = T32(pin=True)
                act(rs[:, 0:512], sb_[:], AF.Ln, [ksb, "dv"], [krs], scale=1.0 / 128, bias=eps_ap)
                yield
                act(rs[:, 0:512], rs[:, 0:512], AF.Exp, [krs], [krs], scale=-0.5)
                yield
                tt(t1[:, 0:512], t1[:, 0:512], rs[:, 0:512], ALU.mult, [kt1, krs], [kt1])
                unpin(krs)
                yield
                tt(mixo[:, 4 + hd, :], t1[:, 0:512], gsb[:, hd, :], ALU.mult, [kt1, f"gsb.{hd}"], [f"mixo.{4 + hd}.{j}" for j in range(4)])
                unpin(kt1)

            ag = [a_chain(c) for c in range(4)]
            bg = [b_chain(h) for h in range(4)]
            bpre = set()

            def bstep(n):
                for _ in range(n):
                    for g in bg[0:2]:
                        if g not in bpre and next(g) == "pre":
                            bpre.add(g)
            rp = rest_proj()
            for na, nb in ((1, 2), (1, 2), (2, 2), (3, 2), (1, 0), (1, 0), (1, 0), (1, 0), (0, 1), (0, 1), (0, 1), (0, 1)):
                next(rp, None)
                for _ in range(na):
                    for g in ag:
                        next(g, None)
                bstep(nb)
            run_rr([rp] + ag)
            while len(bpre) < 2:
                bstep(1)
            Sch1 = split_last(actf[:, 0:5, :].rearrange("p a b -> p (a b)")[:, 0:1152], 9, 128)
            Sb1 = split_last(actb[:, 5:7, :].rearrange("p a b -> p (a b)"), 8, 128)
            kdlo1 = split_last(actb[:, 7, :], 4, 128)
            kdhi1 = split_last(actb[:, 16, :], 4, 128)
            kS1, kB1, kK1 = [f"act.{k}" for k in range(5)], ["act.5", "act.6"], ["act.7", "act.16"]
            p.op(DVE, lambda e: e.memset(kdlo1[64:128], 0.0), writes=["act.7"])
            p.op(DVE, lambda e: e.memset(kdhi1[0:64], 0.0), writes=["act.16"])
            bsets = [(kdTlo, kdThi, Sch, Sb16, ["kdT"], ["Sch"], ["Sb16"]),
                     (kdlo1, kdhi1, Sch1, Sb1, kK1, kS1, kB1)]
            def drive(targets):
                active = dict(targets)
                while active:
                    for g in list(active):
                        v = next(g, "end")
                        if v == "end" or (active[g] is not None and v == active[g]):
                            del active[g]
            drive({bg[0]: "post", bg[1]: "post", bg[2]: "pre", bg[3]: "pre"})
            drive({bg[2]: "post", bg[3]: "post", bg[0]: None, bg[1]: None})
            run_rr(bg)

        def odd_mixer():
            for i in range(2):
                (s,) = w_acquire(1)
                for cc in range(4):
                    g = i * 4 + cc
                    bank, kb = fm_mm(s, 8, cc)
                    act(qfb[:, cc, :] if i == 0 else gsb[:, cc, :], bank[:], AF.Gelu_apprx_tanh, [kb, "par"],
                        [f"qfb.{cc}" if i == 0 else f"gsb.{cc}"], bias=pc("odbu", g))
                w_release(1)
            slots = w_acquire(2)
            vs = {}
            st_, kst = ost, "ost"
            pend = None
            for j in range(4):
                for n in range(2):
                    bank, kb = MM()
                    mm(bank[:], [(xnT[:, k, j * 128:(j + 1) * 128], ring[slots[n]][:, k, :]) for k in range(8)],
                       [f"ring{slots[n]}"] + XNT, [kb])
                    vv, kvv = T32(pin=True)
                    tt(vv[:, 0:512], bank[:], odbv[:, n * 512:(n + 1) * 512], ALU.add, [kb, "odbv"], [kvv])
                    if pend is not None:
                        pend()
                    act(vv[:, 0:512], vv[:, 0:512], AF.Gelu_apprx_tanh, [kvv], [kvv, kst],
                        accum_out=st_[:, 2 * j + n:2 * j + n + 1])
                    vs[j, n] = (vv, kvv)

                    def sumsq(vv=vv, kvv=kvv, j=j, n=n):
                        jk, kjk = T16()
                        p.op(DVE, lambda e: e.scalar_tensor_tensor(
                            out=jk[:], in0=vv[:, 0:512], scalar=1.0, in1=vv[:, 0:512], op0=ALU.mult, op1=ALU.mult,
                            accum_out=st_[:, 8 + 2 * j + n:8 + 2 * j + n + 1]), [kvv], [kjk, kst])
                    pend = sumsq
            pend()
            w_release(2)
            p.op(DVE, lambda e: e.tensor_reduce(out=st_[:, 16:20], in_=split_last(st_[:, 0:8], 4, 2), axis=AX.X,
                                                op=ALU.add), [kst], [kst])
            p.op(DVE, lambda e: e.tensor_reduce(out=st_[:, 20:24], in_=split_last(st_[:, 8:16], 4, 2), axis=AX.X,
                                                op=ALU.add), [kst], [kst])
            ts(st_[:, 16:20], st_[:, 16:20], 1.0 / 1024, None, ALU.mult, None, [kst], [kst])
            tt(st_[:, 24:28], st_[:, 16:20], st_[:, 16:20], ALU.mult, [kst], [kst])
            stt(st_[:, 20:24], st_[:, 20:24], 1.0 / 1024, st_[:, 24:28], ALU.mult, ALU.subtract, [kst], [kst])
            act(st_[:, 20:24], st_[:, 20:24], AF.Ln, [kst, "dv"], [kst], bias=eps_ap)
            act(st_[:, 20:24], st_[:, 20:24], AF.Exp, [kst], [kst], scale=-0.5)
            stt(st_[:, 24:28], st_[:, 16:20], -1.0, st_[:, 20:24], ALU.mult, ALU.mult, [kst], [kst])
            wslots = w_acquire(2)
            ns4 = norm_subs("gffn1")

            def wout(j):
                adds = []
                for n in range(2):
                    bank, kb = MM()
                    mm(bank[:], [(mixo[:, k, j * 128:(j + 1) * 128], ring[wslots[n]][:, k, :]) for k in range(8)],
                       [f"ring{wslots[n]}"] + [f"mixo.{k}.{j}" for k in range(8)], [kb])
                    adds.append((bank, kb, n))
                for bank, kb, n in adds:
                    tt(H[:, j, n * 512:(n + 1) * 512], bank[:], H[:, j, n * 512:(n + 1) * 512], ALU.add,
                       [kb, f"H.{j}"], [f"H.{j}"])
                steps(ns4[j], 3)

            for j in range(4):
                vhs = []
                for n in range(2):
                    vv, kvv = vs[j, n]
                    vh, kvh = T16(pin=True)
                    act(vh[:], vv[:, 0:512], AF.Identity, [kvv, kst], [kvh],
                        scale=st_[:, 20 + j:21 + j], bias=st_[:, 24 + j:25 + j])
                    unpin(kvv)
                    vhs.append((vh, kvh))
                for gq in range(2):
                    vh, kvh = vhs[gq]
                    sv, ksv = MS()
                    mm_multi([(sv[:, gg * 128:(gg + 1) * 128], vh[:, gg * 128:(gg + 1) * 128],
                               WcT[:, gq * 4 + gg, :], True, True) for gg in range(4)], [kvh, "WcT"], [ksv])
                    unpin(kvh)
                    ubuf, un = (qfb, "qfb") if gq == 0 else (gsb, "gsb")
                    tq, ktq = T32()
                    tq3 = split_last(tq[:, 0:512], 4, 128)
                    tt(tq3, split_last(sv[:], 4, 128), bcast_last(pc("odg", 4 * gq, 4), 128), ALU.mult,
                       [ksv, "par"], [ktq])
                    tt(tq3, tq3, Qg[:, 4 * gq:4 * gq + 4, :], ALU.add, [ktq, "Qg"], [ktq])
                    tt(mixo[:, 4 * gq:4 * gq + 4, j * 128:(j + 1) * 128], tq3, ubuf[:, 0:4, j * 128:(j + 1) * 128],
                       ALU.mult, [ktq] + [f"{un}.{c}" for c in range(4)],
                       [f"mixo.{4 * gq + c}.{j}" for c in range(4)], eng=POOL)
                if j >= 1:
                    wout(j - 1)
                if j >= 2:
                    steps(ns4[j - 2], 2)
            wout(3)
            w_release(2)
            steps(ns4[2], 2)
            steps(ns4[3], 2)
            assert all(next(g, "end") == "end" for g in ns4)

        def final_norm(ti):
            r0 = ti * TT
            ss, ks = SM()
            for j in range(4):
                act(junk[:], H[:, j, :], AF.Square, [f"H.{j}"], [ks], accum_out=ss[:, j:j + 1])
            act(ss[:, 4:8], ss[:, 0:4], AF.Ln, [ks, "dv"], [ks], scale=1.0 / D, bias=eps_ap)
            act(ss[:, 4:8], ss[:, 4:8], AF.Exp, [ks], [ks], scale=-0.5)
            for j in range(4):
                for n in range(2):
                    o_, ko = T32()
                    stt(o_[:, 0:512], H[:, j, n * 512:(n + 1) * 512], ss[:, 4 + j:5 + j], gfin[:, n * 512:(n + 1) * 512],
                        ALU.mult, ALU.mult, [f"H.{j}", ks, "gfin"], [ko])
                    p.dma(SP, f"os{2 * j + n}", out[r0 + j * 128:r0 + (j + 1) * 128, n * 512:(n + 1) * 512],
                          o_[:, 0:512], reads=[ko])

        prefetch_x(0)
        run_rr([norm_gen("gmix0", stage)])
        for ti in range(NT):
            stage_to_H()
            last = ti + 1 >= NT
            stages = [even_mixer, proj_then_norm,
                      lambda: None, lambda: ffn(0, ride=norm_subs("gmix1")), lambda: None, odd_mixer]
            for f in stages[:upto]:
                f()
            if not last:
                prefetch_x(ti + 1)
            ffn(1, [] if last else [norm_gen("gmix0", stage)])
            final_norm(ti)
        p.wait_all(SP, [f"os{j}" for j in range(8)])
        p.emit()
    return nc


def make_in_maps(inp, T=SEQ, nb=NB):
    cst, cstb = make_consts()
    par = pack_params(inp)
    f = lambda a: np.ascontiguousarray(np.asarray(a, np.float32))
    shared = {
        "par": par, "cst": cst, "cstb": cstb,
        "gfin": f(inp["norm_final"]).reshape(1, D),
        "odbv": f(inp["od_b_in"][0][1024:]).reshape(1, D),
        "odlb": f(inp["od_ln_b"][0]).reshape(1, D),
        "odbs": f(inp["od_b_s"][0]).reshape(1, 1024),
        "wsT": np.ascontiguousarray(f(inp["od_w_s"][0]).transpose(2, 0, 1).reshape(128, 1024)),
        "ga": f(inp["ev_gate_a_w"][0]), "gx": f(inp["ev_gate_x_w"][0]),
        "w_evin": f(inp["ev_w_in"][0]), "w_evout": f(inp["ev_w_out"][0]),
        "w_odin": f(inp["od_w_in"][0]), "w_odout": f(inp["od_w_out"][0]),
        "w_up0": f(inp["ffn_w_up"][0]), "w_up1": f(inp["ffn_w_up"][1]),
        "w_dn0": f(inp["ffn_w_down"][0]), "w_dn1": f(inp["ffn_w_down"][1]),
    }
    xs = np.asarray(inp["x"], np.float32)
    return [dict(shared, x=np.ascontiguousarray(xs[b, :T])) for b in range(nb)]


def kernel(**inputs):
    nc = build(SEQ)
    in_maps = make_in_maps(inputs, SEQ, NB)
    res = run_bass_kernel_spmd(nc, in_maps, core_ids=list(range(NB)))
    return np.stack([np.asarray(r["out"], np.float32) for r in res.results], axis=0)
```

```python
from contextlib import ExitStack

import ml_dtypes
import numpy as np

import concourse.bass as bass
import concourse.mybir as mybir
from concourse.bass_utils import run_bass_kernel_spmd

F32 = mybir.dt.float32
BF16 = mybir.dt.bfloat16
AF = mybir.ActivationFunctionType
ALU = mybir.AluOpType
AX = mybir.AxisListType

PE, ACT, DVE, POOL, SP = "tensor", "scalar", "vector", "gpsimd", "sync"
ENGINES = (PE, ACT, DVE, POOL, SP)

D = 1024
SEQ = 8192
NB = 8
DFF = 2816
NKF = DFF // 128
TT = 512
EPS = 1e-6
NSLOT = 6


class Prog:
    def __init__(self, nc, stack):
        self.nc = nc
        self.stack = stack
        self.streams = {e: [] for e in ENGINES}
        self.sem = {}
        self.cnt = {}
        self.seen = {e: {} for e in ENGINES}
        self.lastw = {}
        self.readers = {}
        self.gen = {}
        for e in (PE, ACT, DVE, POOL):
            self._mksem(e)

    def _mksem(self, name):
        if name not in self.sem:
            self.sem[name] = self.stack.enter_context(self.nc.semaphore("s_" + name))
            self.cnt[name] = 0

    def alloc(self, base):
        self.gen[base] = self.gen.get(base, 0) + 1
        return f"{base}#{self.gen[base]}"

    def _norm(self, keys):
        out = []
        for k in keys:
            base, _, g = k.partition("#")
            if g:
                assert self.gen.get(base) == int(g), f"stale rotating buffer {k} (now gen {self.gen.get(base)})"
            out.append(base)
        return out

    def _deps(self, eng, reads, writes):
        deps = {}

        def add(tok):
            if tok is None:
                return
            s, v = tok
            if eng == PE and s == PE:
                return
            if deps.get(s, 0) < v:
                deps[s] = v

        for r in reads:
            add(self.lastw.get(r))
        for w in writes:
            add(self.lastw.get(w))
            for t in self.readers.get(w, ()):
                add(t)
        waits = []
        for s, v in deps.items():
            if self.seen[eng].get(s, 0) < v:
                self.seen[eng][s] = v
                waits.append((s, v))
        return waits

    def _commit(self, tok, reads, writes):
        for r in reads:
            self.readers.setdefault(r, []).append(tok)
        for w in writes:
            self.lastw[w] = tok
            self.readers[w] = []

    def op(self, eng, emit, reads=(), writes=()):
        reads, writes = self._norm(reads), self._norm(writes)
        waits = self._deps(eng, reads, writes)
        self.cnt[eng] += 1
        tok = (eng, self.cnt[eng])
        self.streams[eng].append((waits, emit, (eng, 1)))
        self._commit(tok, reads, writes)
        return tok

    def dma(self, queue, lane, out, in_, reads=(), writes=(), **kw):
        self._mksem(lane)
        reads, writes = self._norm(reads), self._norm(writes)
        waits = self._deps(queue, reads, writes)
        self.cnt[lane] += 16
        tok = (lane, self.cnt[lane])
        self.streams[queue].append(
            (waits, lambda e: e.dma_start(out=out, in_=in_, **kw), (lane, 16))
        )
        self._commit(tok, reads, writes)
        return tok

    def wait_all(self, eng, lanes):
        waits = [(l, self.cnt[l]) for l in lanes if self.cnt.get(l, 0) > 0]
        self.streams[eng].append((waits, None, None))

    def emit(self):
        with self.nc.Block() as block:
            for name in ENGINES:
                stream = self.streams[name]

                def body(eng, stream=stream):
                    for waits, emit, inc in stream:
                        for s, v in waits:
                            eng.wait_ge(self.sem[s], v)
                        if emit is not None:
                            ins = emit(eng)
                            ins.then_inc(self.sem[inc[0]], inc[1])

                getattr(block, name)(body)


def bcast_last(ap, m):
    return bass.AP(ap.tensor, ap.offset, [list(x) for x in ap.ap] + [[0, m]])


def bcast_mid(ap, m):
    a = [list(x) for x in ap.ap]
    return bass.AP(ap.tensor, ap.offset, [a[0], [0, m]] + a[1:])


def split_last(ap, a, b):
    l = [list(x) for x in ap.ap]
    assert l[-1][1] == a * b
    st = l[-1][0]
    return bass.AP(ap.tensor, ap.offset, l[:-1] + [[st * b, a], [st, b]])


PC = {}
_o = 0
for _n, _w in (("gmix0", 8), ("gffn0", 8), ("gmix1", 8), ("gffn1", 8), ("evcw", 16), ("evcb", 4),
               ("gab", 4), ("gxb", 4), ("lam", 4), ("hgl", 12), ("hgn", 1), ("odbu", 8), ("odg", 8),
               ("ffcw", 132), ("ffcb", 44)):
    PC[_n] = _o
    _o += _w
NPAR = _o


def fm(v, nch):
    return np.ascontiguousarray(np.asarray(v, np.float32).reshape(nch, 128).T)


def pack_params(inp):
    par = np.zeros((128, NPAR), np.float32)

    def put(name, arr):
        par[:, PC[name]:PC[name] + arr.shape[1]] = arr

    put("gmix0", fm(inp["norm_mix"][0], 8))
    put("gmix1", fm(inp["norm_mix"][1], 8))
    put("gffn0", fm(inp["norm_ffn"][0], 8))
    put("gffn1", fm(inp["norm_ffn"][1], 8))
    cw = np.asarray(inp["ev_conv_w"][0], np.float32)
    put("evcw", np.ascontiguousarray(cw.reshape(4, 4, 128).transpose(2, 1, 0).reshape(128, 16)))
    put("evcb", fm(inp["ev_conv_b"][0], 4))
    put("gab", fm(inp["ev_gate_a_b"][0], 4))
    put("gxb", fm(inp["ev_gate_x_b"][0], 4))
    put("lam", fm(inp["ev_lru_lambda"][0], 4))
    hl = np.asarray(inp["hg_lb_logits"], np.float32)
    put("hgl", np.ascontiguousarray(hl.reshape(3, 4, 128).transpose(2, 1, 0).reshape(128, 12)))
    put("hgn", np.asarray(inp["ev_hg_norm"][0], np.float32).reshape(128, 1))
    put("odbu", fm(inp["od_b_in"][0][:1024], 8))
    put("odg", fm(inp["od_ln_g"][0], 8))
    fw = np.asarray(inp["ffn_conv_w"], np.float32)
    put("ffcw", np.ascontiguousarray(fw.reshape(2, 3, NKF, 128).transpose(3, 0, 2, 1).reshape(128, 132)))
    fb = np.asarray(inp["ffn_conv_b"], np.float32)
    put("ffcb", np.ascontiguousarray(fb.reshape(2, NKF, 128).transpose(2, 0, 1).reshape(128, 44)))
    return par


def make_consts():
    s = np.arange(128)[:, None]
    t = np.arange(128)[None, :]
    tril = (s <= t).astype(np.float32)
    m2 = ((s <= t) & ((s // 64) == (t // 64))).astype(np.float32)
    m01 = np.ones((128, TT), np.float32)
    m01[:, ::64] = 0.0
    cst = np.concatenate([tril, m2, m01], axis=1)
    cstb = np.concatenate([np.eye(128), np.ones((128, 128))], axis=1).astype(ml_dtypes.bfloat16)
    return cst, cstb


def build(T=SEQ, upto=99):
    NT = T // TT
    nc = bass.Bass("TRN2", target_bir_lowering=False)

    def din(name, shape, dt=F32):
        return nc.dram_tensor(name, list(shape), dt, kind="ExternalInput").ap()

    x = din("x", [T, D])
    out = nc.dram_tensor("out", [T, D], F32, kind="ExternalOutput").ap()
    par_d = din("par", [128, NPAR])
    cst_d = din("cst", [128, 768])
    cstb_d = din("cstb", [128, 256], BF16)
    gfin_d = din("gfin", [1, D])
    odbv_d = din("odbv", [1, D])
    odlb_d = din("odlb", [1, D])
    odbs_d = din("odbs", [1, 1024])
    wsT_d = din("wsT", [128, 1024])
    ga_d = din("ga", [8, 64, 64])
    gx_d = din("gx", [8, 64, 64])
    W_evin = din("w_evin", [D, 3072])
    W_evout = din("w_evout", [D, D])
    W_odin = din("w_odin", [D, 2048])
    W_odout = din("w_odout", [D, D])
    W_up = [din("w_up0", [D, 2 * DFF]), din("w_up1", [D, 2 * DFF])]
    W_dn = [din("w_dn0", [DFF, D]), din("w_dn1", [DFF, D])]
    NPT = 48
    wscr = nc.dram_tensor("wscr", [NPT, 128, 4096], BF16, kind="Internal").ap()

    with ExitStack() as st:
        def sb(name, shape, dt=F32):
            return st.enter_context(nc.sbuf_tensor("sb_" + name, list(shape), dt))

        def ps(name, shape, dt=F32):
            return st.enter_context(nc.psum_tensor("ps_" + name, list(shape), dt))

        p = Prog(nc, st)

        H = sb("H", [128, 4, D])
        xn_tm = [sb(f"xntm{i}", [128, D], BF16) for i in range(2)]
        xnT = sb("xnT", [128, 8, TT], BF16)
        mixo = sb("mixo", [128, 8, TT], BF16)
        actb = sb("actb", [128, NKF, TT], BF16)
        ring = [sb(f"ring{i}", [128, 8, 512], BF16) for i in range(NSLOT)]
        NT32, NT16 = 14, 16
        t32 = [sb(f"t32_{i}", [128, 516]) for i in range(NT32)]
        t16 = [sb(f"t16_{i}", [128, 512], BF16) for i in range(NT16)]
        ygb = sb("ygb", [128, 4, TT], BF16)
        qfb = sb("qfb", [128, 4, TT], BF16)
        gsb = sb("gsb", [128, 4, TT], BF16)
        vtm = sb("vtm", [128, 4, 512], BF16)
        kdTlo = sb("kdTlo", [128, 4, 128], BF16)
        kdThi = sb("kdThi", [128, 4, 128], BF16)
        Sch = sb("Sch", [128, 9, 128])
        Sb16 = sb("Sb16", [128, 8, 128], BF16)
        Scar = sb("Scar", [128, 4, 128])
        hcar = sb("hcar", [128, 4])
        xtail = sb("xtail", [128, 4, 3])
        gtail = sb("gtail", [128, 2, NKF, 2])
        cst = sb("cst", [128, 768])
        cstb = sb("cstb", [128, 256], BF16)
        par = sb("par", [128, NPAR])
        gfin = sb("gfin", [128, D])
        odbv = sb("odbv", [128, D])
        Qg = sb("Qg", [128, 8, 128])
        WcT = sb("WcT", [128, 8, 128], BF16)
        WcT32 = sb("WcT32", [128, 8, 128])
        BD = sb("BD", [128, 8, 128], BF16)
        dv = sb("dv", [128, 32])
        ost = sb("ost", [128, 32])
        nst = sb("nst", [128, 16])
        nst_slot = [0]
        junk = sb("junk", [128, D], BF16)
        junk2 = sb("junk2", [128, D], BF16)
        NSM = 24
        sm = [sb(f"sm{i}", [128, 8]) for i in range(NSM)]

        mmb = [ps(f"mm{i}", [128, 512]) for i in range(4)]
        trb = [ps(f"tr{i}", [128, 8, 128], BF16) for i in range(2)]
        msb = [ps(f"ms{i}", [128, 512]) for i in range(2)]

        eps_ap = dv[:, 20:21]
        one_ap = dv[:, 21:22]
        tril = cst[:, 0:128]
        m2 = cst[:, 128:256]
        m01 = cst[:, 256:768]
        ident = cstb[:, 0:128]
        ones_b = cstb[:, 128:256]

        rot = {"t32": 0, "t16": 0, "mm": 0, "ms": 0, "sm": 0, "tr": 0}

        pinned = set()

        def T32(pin=False):
            for _ in range(NT32):
                i = rot["t32"] % NT32
                rot["t32"] += 1
                if i not in pinned:
                    break
            else:
                raise AssertionError("all t32 temps pinned")
            if pin:
                pinned.add(i)
            return t32[i], p.alloc(f"t32_{i}")

        def unpin(key):
            base = key.split("#")[0]
            (pinned if base.startswith("t32") else pinned16).discard(int(base.split("_")[1]))

        pinned16 = set()

        def T16(pin=False):
            for _ in range(NT16):
                i = rot["t16"] % NT16
                rot["t16"] += 1
                if i not in pinned16:
                    break
            else:
                raise AssertionError("all t16 temps pinned")
            if pin:
                pinned16.add(i)
            return t16[i], p.alloc(f"t16_{i}")

        def MM():
            i = rot["mm"] % 4
            rot["mm"] += 1
            return mmb[i], p.alloc(f"mm{i}")

        def MMX():
            i = rot.setdefault("mmx", 0) % 6
            rot["mmx"] += 1
            return (mmb[i], p.alloc(f"mm{i}")) if i < 4 else (msb[i - 4], p.alloc(f"ms{i - 4}"))

        def MS():
            i = rot["ms"] % 2
            rot["ms"] += 1
            return msb[i], p.alloc(f"ms{i}")

        def SM():
            i = rot["sm"] % NSM
            rot["sm"] += 1
            return sm[i], p.alloc(f"sm{i}")

        def TR():
            i = rot["tr"] % 2
            rot["tr"] += 1
            return trb[i], p.alloc(f"tr{i}")

        def pc(name, i=0, n=1):
            o = PC[name] + i
            return par[:, o:o + n]

        def act(out_, in_, func, reads, writes, eng=ACT, **kw):
            p.op(eng, lambda e: e.activation(out=out_, in_=in_, func=func, **kw), reads, writes)

        def tt(out_, in0, in1, op, reads, writes, eng=DVE):
            p.op(eng, lambda e: e.tensor_tensor(out=out_, in0=in0, in1=in1, op=op), reads, writes)

        def ts(out_, in0, s1, s2, op0, op1, reads, writes, eng=DVE):
            if s2 is None:
                p.op(eng, lambda e: e.tensor_scalar(out=out_, in0=in0, scalar1=s1, scalar2=None, op0=op0),
                     reads, writes)
            else:
                p.op(eng, lambda e: e.tensor_scalar(out=out_, in0=in0, scalar1=s1, scalar2=s2, op0=op0, op1=op1),
                     reads, writes)

        def stt(out_, in0, scalar, in1, op0, op1, reads, writes):
            p.op(DVE, lambda e: e.scalar_tensor_tensor(out=out_, in0=in0, scalar=scalar, in1=in1, op0=op0, op1=op1),
                 reads, writes)

        def cp(out_, in_, reads, writes, eng=ACT):
            if eng == ACT:
                p.op(ACT, lambda e: e.copy(out=out_, in_=in_), reads, writes)
            else:
                p.op(eng, lambda e: e.tensor_copy(out=out_, in_=in_), reads, writes)

        def mm(out_, pairs, reads, writes):
            def emit(e):
                n = len(pairs)
                ins = None
                for i, (l, r) in enumerate(pairs):
                    ins = e.matmul(out_, l, r, start=(i == 0), stop=(i == n - 1))
                return ins
            p.op(PE, emit, reads, writes)

        def mm_multi(items, reads, writes):
            def emit(e):
                ins = None
                for (o, l, r, s0, s1) in items:
                    ins = e.matmul(o, l, r, start=s0, stop=s1)
                return ins
            p.op(PE, emit, reads, writes)

        def run_rr(gens):
            gens = list(gens)
            while gens:
                for g in list(gens):
                    try:
                        next(g)
                    except StopIteration:
                        gens.remove(g)

        XNT = [f"xnT.{j}" for j in range(4)]
        MIXO = [f"mixo.{k}.{j}" for k in range(8) for j in range(4)]

        seq = []

        def pieces(W, ncols_list, kchunks=8, k0=0):
            return [(W, k0, kchunks, n0) for n0 in ncols_list]

        per_tile = []
        per_tile += pieces(W_evin, [512, 1536, 0, 2048, 1024, 2560])
        per_tile += pieces(W_evout, [0, 512])
        per_tile += pieces(W_up[0], [i * 512 for i in range(11)])
        for n in range(2):
            per_tile += [(W_dn[0], 0, 8, n * 512), (W_dn[0], 8, 8, n * 512), (W_dn[0], 16, 6, n * 512)]
        per_tile += pieces(W_odin, [0, 512, 1024, 1536])
        per_tile += pieces(W_odout, [0, 512])
        per_tile += pieces(W_up[1], [i * 512 for i in range(11)])
        for n in range(2):
            per_tile += [(W_dn[1], 0, 8, n * 512), (W_dn[1], 8, 8, n * 512), (W_dn[1], 16, 6, n * 512)]
        assert len(per_tile) == NPT
        seq = per_tile * NT
        wst = {"next": 0, "issued": 0, "done": 0}

        def w_issue():
            while wst["issued"] < min(len(seq), wst["done"] + NSLOT):
                i = wst["issued"]
                W, k0, nk, n0 = seq[i]
                s = i % NSLOT
                if i < NPT:
                    src = W[k0 * 128:(k0 + nk) * 128, n0:n0 + 512].rearrange("(k p) n -> p k n", p=128)
                    p.dma(POOL, f"w{s}", ring[s][:, 0:nk, :], src, writes=[f"ring{s}"])
                else:
                    p.dma(POOL, f"w{s}", ring[s][:, 0:nk, :].rearrange("p k n -> p (k n)"),
                          wscr[i % NPT][:, 0:nk * 512], reads=[f"scr{i % NPT}"], writes=[f"ring{s}"])
                wst["issued"] += 1

        def w_acquire(n=1):
            idx = list(range(wst["next"], wst["next"] + n))
            wst["next"] += n
            assert idx[-1] < wst["issued"], "weight piece not issued"
            for i in idx:
                if i < NPT and NT > 1:
                    nk = seq[i][2]
                    s = i % NSLOT
                    p.dma(SP, f"ws{s}", wscr[i][:, 0:nk * 512], ring[s][:, 0:nk, :].rearrange("p k n -> p (k n)"),
                          reads=[f"ring{s}"], writes=[f"scr{i}"])
            return [i % NSLOT for i in idx]

        def w_release(n=1):
            wst["done"] += n
            w_issue()

        p.dma(SP, "c0", par[:], par_d, writes=["par"])
        p.dma(SP, "c1", cst[:], cst_d, writes=["cst"])
        p.dma(SP, "c2", cstb[:], cstb_d, writes=["cstb"])
        p.dma(SP, "c3", gfin[:], bass.AP(gfin_d.tensor, 0, [[0, 128], [1, D]]), writes=["gfin"])
        p.dma(SP, "c4", odbv[:], bass.AP(odbv_d.tensor, 0, [[0, 128], [1, D]]), writes=["odbv"])
        p.dma(SP, "c5", Qg[:].rearrange("p g t -> p (g t)"), bass.AP(odbs_d.tensor, 0, [[0, 128], [1, 1024]]),
              writes=["Qg"])
        p.dma(SP, "c6", WcT32[:].rearrange("p g t -> p (g t)"), wsT_d, writes=["WcT32"])
        lbA, kA = T32()
        lbB, kB = T32()
        p.dma(SP, "c7", lbA[:, 0:512], bass.AP(odlb_d.tensor, 0, [[0, 128], [1, 512]]), writes=[kA])
        p.dma(SP, "c8", lbB[:, 0:512], bass.AP(odlb_d.tensor, 512, [[0, 128], [1, 512]]), writes=[kB])
        p.op(DVE, lambda e: e.memset(Scar[:], 0.0), writes=[f"Scar.{h}" for h in range(4)])
        p.op(DVE, lambda e: e.memset(hcar[:], 0.0), writes=[f"hcar.{c}" for c in range(4)])
        p.op(DVE, lambda e: e.memset(xtail[:], 0.0), writes=[f"xtail.{c}" for c in range(4)])
        p.op(DVE, lambda e: e.memset(gtail[:], 0.0), writes=[f"gtail.{l}.{j}" for l in range(2) for j in range(NKF)])
        p.op(DVE, lambda e: e.memset(BD[:], 0.0), writes=["BD"])
        p.op(DVE, lambda e: e.memset(kdTlo[:], 0.0), writes=["kdT"])
        p.op(DVE, lambda e: e.memset(kdThi[:], 0.0), writes=["kdT"])
        p.op(DVE, lambda e: e.memset(dv[:, 20:21], EPS), writes=["dv"])
        p.op(DVE, lambda e: e.memset(dv[:, 21:22], 1.0), writes=["dv"])
        for c in range(4):
            for gi, gd in enumerate((ga_d, gx_d)):
                for hh in range(2):
                    p.dma(POOL, f"bd{gi}{hh}", BD[hh * 64:(hh + 1) * 64, gi * 4 + c, hh * 64:(hh + 1) * 64],
                          gd[2 * c + hh], reads=[], writes=["BD"])
        w_issue()
        tt(WcT32[:], WcT32[:], bcast_mid(tril, 8), ALU.mult, ["WcT32", "cst"], ["WcT32"])
        cp(WcT[:], WcT32[:], ["WcT32"], ["WcT"], eng=DVE)
        for g in range(8):
            bank, kb = MS()
            lbt = (lbA if g < 4 else lbB)[:, (g % 4) * 128:(g % 4 + 1) * 128]
            kl = kA if g < 4 else kB
            mm(bank[:, 0:128], [(lbt, WcT32[:, g, :])], [kl, "WcT32"], [kb])
            tt(Qg[:, g, :], Qg[:, g, :], bank[:, 0:128], ALU.add, ["Qg", kb], ["Qg"])
        e12, ke = SM()
        e12b = sb("e12b", [128, 12])
        act(e12b[:], pc("hgl", 0, 12), AF.Exp, ["par"], ["e12b"])
        p.op(DVE, lambda e: e.tensor_reduce(out=e12[:, 0:4], in_=split_last(e12b[:], 4, 3), axis=AX.X, op=ALU.add),
             ["e12b"], [ke])
        p.op(DVE, lambda e: e.reciprocal(out=e12[:, 4:8], in_=e12[:, 0:4]), [ke], [ke])
        tt(dv[:, 0:4], e12b[:, 0:12:3], e12[:, 4:8], ALU.mult, ["e12b", ke], ["dv"])
        ts(dv[:, 4:8], dv[:, 0:4], -1.0, 1.0, ALU.mult, ALU.add, ["dv"], ["dv"])
        ts(dv[:, 8:12], dv[:, 0:4], -1.0, None, ALU.add, None, ["dv"], ["dv"])
        s1, k1 = SM()
        act(s1[:, 0:4], pc("lam", 0, 4), AF.Exp, ["par"], [k1], scale=-1.0)
        act(s1[:, 4:8], s1[:, 0:4], AF.Ln, [k1, "dv"], [k1], bias=one_ap)
        ts(dv[:, 12:16], s1[:, 4:8], -8.0, None, ALU.mult, None, [k1], ["dv"])
        ts(dv[:, 16:20], s1[:, 4:8], -16.0, None, ALU.mult, None, [k1], ["dv"])

        stage = []
        for buf, nm in ((ygb, "ygb"), (qfb, "qfb"), (gsb, "gsb"), (vtm, "vtm")):
            stage.append((buf.bitcast(F32)[:].rearrange("p a b -> p (a b)"), [f"{nm}.{c}" for c in range(4)]))

        def prefetch_x(ti):
            r0 = ti * TT
            for j in range(4):
                p.dma(SP, f"xl{j}", stage[j][0], x[r0 + j * 128:r0 + (j + 1) * 128, :], writes=stage[j][1])

        def stage_to_H():
            for j in range(4):
                p.dma(SP, f"xc{j}", H[:, j, :], stage[j][0], reads=stage[j][1], writes=[f"H.{j}"])

        def norm_subs(gname, srcs=None):
            if srcs is None:
                srcs = [(H[:, j, :], [f"H.{j}"]) for j in range(4)]
            slot = nst_slot[0] % 2
            nst_slot[0] += 1
            ss = nst[:, slot * 8:(slot + 1) * 8]
            gb = bcast_last(pc(gname, 0, 8), 128)

            def sub(j):
                src, sk = srcs[j]
                ks = f"nst{slot}.{j}"
                xt, kx = xn_tm[j % 2], p.alloc(f"xntm{j % 2}")
                act(junk[:], src, AF.Square, sk, [ks], accum_out=ss[:, j:j + 1])
                yield
                act(ss[:, 4 + j:5 + j], ss[:, j:j + 1], AF.Ln, [ks, "dv"], [ks], scale=1.0 / D, bias=eps_ap)
                act(ss[:, 4 + j:5 + j], ss[:, 4 + j:5 + j], AF.Exp, [ks], [ks], scale=-0.5)
                yield
                if j % 2 == 0:
                    ts(xt[:], src, ss[:, 4 + j:5 + j], None, ALU.mult, None, sk + [ks], [kx])
                else:
                    act(xt[:], src, AF.Copy, sk + [ks], [kx], scale=ss[:, 4 + j:5 + j])
                yield
                tr, kt = TR()

                def emit(e):
                    ins = None
                    for k in range(8):
                        ins = e.transpose(tr[:, k, :], xt[:, k * 128:(k + 1) * 128], ident)
                    return ins
                p.op(PE, emit, [kx, "cstb"], [kt])
                tt(xnT[:, :, j * 128:(j + 1) * 128], tr[:], gb, ALU.mult, [kt, "par"], [f"xnT.{j}"])
            return [sub(j) for j in range(4)]

        def norm_gen(gname, srcs=None):
            for g in norm_subs(gname, srcs):
                for _ in g:
                    yield
                yield

        def steps(g, n):
            for _ in range(n):
                next(g, None)

        def rmsnorm_T(gname):
            run_rr([norm_gen(gname)])

        def fm_mm(slot, nk, cc):
            bank, kb = MMX()
            mm(bank[:], [(ring[slot][:, k, cc * 128:(cc + 1) * 128], xnT[:, k, :]) for k in range(nk)],
               [f"ring{slot}"] + XNT, [kb])
            return bank, kb

        def proj_gen(src, srckeys, nk_total, ride=None):
            npc = (nk_total + 7) // 8
            for n in range(2):
                banks = [MM() for _ in range(4)]
                for pi in range(npc):
                    (slot,) = w_acquire(1)
                    ks = list(range(pi * 8, min(nk_total, pi * 8 + 8)))
                    for j in range(4):
                        bank, kb = banks[j]
                        mm_multi([(bank[:], src[:, k, j * 128:(j + 1) * 128], ring[slot][:, k % 8, :],
                                   k == 0, k == nk_total - 1) for k in ks],
                                 [f"ring{slot}"] + srckeys, [kb])
                        if pi == npc - 1:
                            tt(H[:, j, n * 512:(n + 1) * 512], bank[:], H[:, j, n * 512:(n + 1) * 512], ALU.add,
                               [kb, f"H.{j}"], [f"H.{j}"])
                            if ride is not None and n == 1:
                                if j >= 2:
                                    steps(ride[j - 2], 2)
                                steps(ride[j], 3)
                        yield
                    w_release(1)
            if ride is not None:
                steps(ride[2], 2)
                steps(ride[3], 2)
                assert all(next(g, "end") == "end" for g in ride)

        def proj_blocks_gen():
            wsl = w_acquire(2)
            for j in range(4):
                for n in range(2):
                    bank, kb = MM()
                    mm(bank[:], [(mixo[:, k, j * 128:(j + 1) * 128], ring[wsl[n]][:, k, :]) for k in range(8)],
                       [f"ring{wsl[n]}"] + [f"mixo.{k}.{j}" for k in range(8)], [kb])
                    tt(H[:, j, n * 512:(n + 1) * 512], bank[:], H[:, j, n * 512:(n + 1) * 512], ALU.add,
                       [kb, f"H.{j}"], [f"H.{j}"])
                    yield
            w_release(2)

        def proj_then_norm():
            pg, ns = proj_blocks_gen(), norm_subs("gffn0")
            for j in range(4):
                steps(pg, 2)
                steps(ns[j], 3)
                if j >= 1:
                    steps(ns[j - 1], 2)
            steps(pg, 1)
            steps(ns[3], 2)
            assert all(next(g, "end") == "end" for g in ns + [pg])

        def proj_residual(src, srckeys, nk_total, side=(), ride=None):
            run_rr([proj_gen(src, srckeys, nk_total, ride)] + list(side))

        def ffn(l, side=(), ride=None):
            AK = [f"act.{k}" for k in range(NKF)]
            pend = None
            for i in range(11):
                (slot,) = w_acquire(1)
                for cc in range(4):
                    cg = i * 4 + cc
                    bank, kb = fm_mm(slot, 8, cc)
                    if cg < NKF:
                        j = cg
                        ktl = f"gtail.{l}.{j}"
                        a0, ka = T32()
                        wo = PC["ffcw"] + l * 66 + j * 3
                        bo = PC["ffcb"] + l * NKF + j
                        w0, w1, w2 = par[:, wo:wo + 1], par[:, wo + 1:wo + 2], par[:, wo + 2:wo + 3]
                        bb = par[:, bo:bo + 1]
                        act(a0[:, 0:2], gtail[:, l, j, :], AF.Identity, [ktl, "par"], [ka], scale=w0, bias=bb)
                        act(a0[:, 2:512], bank[:, 0:510], AF.Identity, [kb, "par"], [ka], scale=w0, bias=bb)
                        stt(a0[:, 0:1], gtail[:, l, j, 1:2], w1, a0[:, 0:1], ALU.mult, ALU.add, [ktl, ka, "par"], [ka])
                        stt(a0[:, 1:512], bank[:, 0:511], w1, a0[:, 1:512], ALU.mult, ALU.add, [kb, ka, "par"], [ka])
                        stt(a0[:, 0:512], bank[:, 0:512], w2, a0[:, 0:512], ALU.mult, ALU.add, [kb, ka, "par"], [ka])
                        cp(gtail[:, l, j, :], bank[:, 510:512], [kb], [ktl], eng=DVE)
                        if pend is not None:
                            pend()
                        pend = (lambda a0=a0, ka=ka, j=j:
                                act(actb[:, j, :], a0[:, 0:512], AF.Silu, [ka], [f"act.{j}"]))
                    else:
                        if pend is not None:
                            pend()
                            pend = None
                        j = cg - NKF
                        tt(actb[:, j, :], bank[:], actb[:, j, :], ALU.mult, [kb, f"act.{j}"], [f"act.{j}"])
                w_release(1)
            proj_residual(actb, AK, NKF, side, ride)

        actf = actb.bitcast(F32)

        def actv(k0):
            return actf[:, k0:k0 + 2, :].rearrange("p a b -> p (a b)"), [f"act.{k0}", f"act.{k0 + 1}"]

        def even_mixer():
            s_x, s_f = w_acquire(2)
            xcs, sgs = [], []
            pcast = None
            for c in range(4):
                bank, kb = fm_mm(s_x, 8, c)
                ktl = f"xtail.{c}"
                xc, kc = actv(2 * c)
                wo = PC["evcw"] + c * 4
                w = [par[:, wo + k:wo + k + 1] for k in range(4)]
                act(xc[:, 0:3], xtail[:, c, :], AF.Identity, [ktl, "par"], kc, scale=w[0], bias=pc("evcb", c))
                act(xc[:, 3:512], bank[:, 0:509], AF.Identity, [kb, "par"], kc, scale=w[0], bias=pc("evcb", c))
                for k in range(1, 4):
                    if k < 3:
                        stt(xc[:, 0:3 - k], xtail[:, c, k:3], w[k], xc[:, 0:3 - k], ALU.mult, ALU.add,
                            [ktl, "par"] + kc, kc)
                    stt(xc[:, 3 - k:512], bank[:, 0:509 + k], w[k], xc[:, 3 - k:512], ALU.mult, ALU.add,
                        [kb, "par"] + kc, kc)
                cp(xtail[:, c, :], bank[:, 509:512], [kb], [ktl], eng=DVE)
                hd = c
                bank, kb = fm_mm(s_f, 8, hd)
                sg, ksg = actv(8 + 2 * hd)
                act(sg[:], bank[:], AF.Sigmoid, [kb], ksg)
                sgs.append((sg, ksg))
                if pcast is not None:
                    pcast()

                def pcast(xc=xc, kc=kc):
                    xcb, kcb = T16(pin=True)
                    cp(xcb[:], xc[:, 0:512], kc, [kcb])
                    xcs.append((xc, kc, xcb, kcb))
            pcast()
            w_release(2)
            (s_y,) = w_acquire(1)
            for c in range(4):
                bank, kb = fm_mm(s_y, 8, c)
                act(ygb[:, c, :], bank[:], AF.Gelu_apprx_tanh, [kb], [f"ygb.{c}"])
            w_release(1)

            def rest_proj():
                (s_v,) = w_acquire(1)
                for j in range(4):
                    bank, kb = MM()
                    mm(bank[:], [(xnT[:, k, j * 128:(j + 1) * 128], ring[s_v][:, k, :]) for k in range(8)],
                       [f"ring{s_v}"] + XNT, [kb])
                    cp(vtm[:, j, :], bank[:], [kb], [f"vtm.{j}"])
                    yield
                w_release(1)
                (s_q,) = w_acquire(1)
                for hd in range(4):
                    bank, kb = fm_mm(s_q, 8, hd)
                    act(qfb[:, hd, :], bank[:], AF.Silu, [kb], [f"qfb.{hd}"])
                    yield
                w_release(1)
                (s_g,) = w_acquire(1)
                for hd in range(4):
                    bank, kb = fm_mm(s_g, 8, hd)
                    act(gsb[:, hd, :], bank[:], AF.Silu, [kb], [f"gsb.{hd}"])
                    yield
                w_release(1)

            def a_chain(c):
                xc, kc, xcb, kcb = xcs[c]
                br, kbr = MM()
                mm(br[:], [(BD[:, c, :], xcb[:])], ["BD", kcb], [kbr])
                r, kr = T32(pin=True)
                act(r[:, 0:512], br[:], AF.Sigmoid, [kbr, "par"], [kr], bias=pc("gab", c))
                yield
                bi, kbi = MM()
                mm(bi[:], [(BD[:, 4 + c, :], xcb[:])], ["BD", kcb], [kbi])
                unpin(kcb)
                gi, kgi = T32(pin=True)
                act(gi[:, 0:512], bi[:], AF.Sigmoid, [kbi, "par"], [kgi], bias=pc("gxb", c))
                yield
                a, ka = T32(pin=True)
                act(a[:, 0:512], r[:, 0:512], AF.Exp, [kr, "dv"], [ka], scale=dv[:, 12 + c:13 + c])
                yield
                tt(gi[:, 0:512], gi[:, 0:512], xc[:, 0:512], ALU.mult, [kgi] + kc, [kgi])
                yield
                act(r[:, 0:512], r[:, 0:512], AF.Exp, [kr, "dv"], [kr], scale=dv[:, 16 + c:17 + c])
                yield
                act(r[:, 0:512], r[:, 0:512], AF.Ln, [kr, "dv"], [kr], scale=-1.0, bias=one_ap)
                yield
                act(r[:, 0:512], r[:, 0:512], AF.Exp, [kr], [kr], scale=0.5)
                yield
                tt(gi[:, 0:512], gi[:, 0:512], r[:, 0:512], ALU.mult, [kgi, kr], [kgi])
                yield
                p.op(DVE, lambda e: e.tensor_tensor_scan(
                    out=r[:, 0:512], data0=a[:, 0:512], data1=gi[:, 0:512], initial=hcar[:, c:c + 1],
                    op0=ALU.mult, op1=ALU.add), [ka, kgi, f"hcar.{c}"], [kr])
                cp(hcar[:, c:c + 1], r[:, 511:512], [kr], [f"hcar.{c}"], eng=DVE)
                yield
                tt(mixo[:, c, :], r[:, 0:512], ygb[:, c, :], ALU.mult, [kr, f"ygb.{c}"], [f"mixo.{c}.{j}" for j in range(4)])
                unpin(kr)
                unpin(kgi)
                unpin(ka)

            def b_chain(hd):
                sg, ksg = sgs[hd]
                lb, oml, noml = dv[:, hd:hd + 1], dv[:, 4 + hd:5 + hd], dv[:, 8 + hd:9 + hd]
                if hd < 2:
                    lf, klf_l = actv(17 + 2 * hd)
                    klf = None
                else:
                    lf, klf = T32(pin=True)
                    klf_l = [klf]
                act(lf[:, 0:512], sg[:], AF.Ln, ksg + ["dv"], klf_l, scale=oml, bias=lb)
                yield
                act(sg[:], sg[:], AF.Identity, ksg + ["dv"], ksg, scale=noml, bias=oml)
                yield
                bc_, kbc = T32(pin=True)
                p.op(DVE, lambda e: e.tensor_tensor_scan(
                    out=bc_[:, 0:512], data0=m01, data1=lf[:, 0:512], initial=0.0, op0=ALU.mult, op1=ALU.add),
                    klf_l + ["cst"], [kbc])
                yield
                bmid = bc_[:, 31:512:64]
                blast = bc_[:, 63:512:64]
                tt(split_last(lf[:, 0:512], 8, 64), split_last(bc_[:, 0:512], 8, 64), bcast_last(bmid, 64),
                   ALU.subtract, [kbc], klf_l)
                sx, ksx = SM()
                sy, ksy = SM()
                sz, ksz = SM()
                yield
                act(sx[:, 0:8], bmid, AF.Exp, [kbc], [ksx])
                tt(sy[:, 0:8], blast, bmid, ALU.subtract, [kbc], [ksy])
                yield
                act(sy[:, 0:8], sy[:, 0:8], AF.Exp, [ksy], [ksy])
                act(sz[:, 0:8], blast, AF.Exp, [kbc], [ksz])
                yield
                act(bc_[:, 0:512], lf[:, 0:512], AF.Exp, klf_l, [kbc])
                yield
                act(lf[:, 0:512], lf[:, 0:512], AF.Exp, klf_l, klf_l, scale=-1.0)
                yield
                qd, kqd = T16(pin=True)
                tt(qd[:], bc_[:, 0:512], qfb[:, hd, :], ALU.mult, [kbc, f"qfb.{hd}"], [kqd])
                unpin(kbc)
                yield
                kd, kkd = T16(pin=True)
                tt(kd[:], sg[:], lf[:, 0:512], ALU.mult, ksg + klf_l, [kkd])
                if klf is not None:
                    unpin(klf)
                yield
                qi, kqi = T16(pin=True)
                tt(split_last(qi[:], 8, 64), split_last(qd[:], 8, 64), bcast_last(sx[:, 0:8], 64), ALU.mult,
                   [kqd, ksx], [kqi], eng=POOL)
                yield
                kc_, kkc = T16(pin=True)
                tt(split_last(kc_[:], 8, 64), split_last(kd[:], 8, 64), bcast_last(sy[:, 0:8], 64), ALU.mult,
                   [kkd, ksy], [kkc])
                yield
                sc, ksc = MS()
                mm_multi([(sc[:, j * 128:(j + 1) * 128], kd[:, j * 128:(j + 1) * 128], qd[:, j * 128:(j + 1) * 128],
                           True, True) for j in range(4)], [kkd, kqd], [ksc])
                unpin(kkd)
                unpin(kqd)
                ms_, kms = T16(pin=True)
                tt(split_last(ms_[:], 4, 128), split_last(sc[:], 4, 128), bcast_mid(m2, 4), ALU.mult,
                   [ksc, "cst"], [kms])
                yield
                yield "pre"
                kdTlo_, kdThi_, Sch_, Sb16_, kKd, kSch, kSb = bsets[hd % 2]
                tr, kt = TR()

                def emit(e, tr=tr, kc_=kc_):
                    ins = None
                    for j in range(4):
                        ins = e.transpose(tr[:, j, :], kc_[:, j * 128:(j + 1) * 128], ident)
                    return ins
                p.op(PE, emit, [kkc, "cstb"], [kt])
                unpin(kkc)
                cp(kdTlo_[0:64], tr[0:64, 0:4, :], [kt], kKd)
                cp(kdThi_[64:128], tr[64:128, 0:4, :], [kt], kKd)
                cp(Sch_[:, 0, :], Scar[:, hd, :], [f"Scar.{hd}"], kSch)
                yield
                kvs = []
                for half_t in range(2):
                    kv, kkv = MS()
                    items = []
                    for cq in range(4):
                        cidx = half_t * 4 + cq
                        j, hh = cidx // 2, cidx % 2
                        items.append((kv[:, cq * 128:(cq + 1) * 128], (kdThi_ if hh else kdTlo_)[:, j, :],
                                      vtm[:, j, hd * 128:(hd + 1) * 128], True, True))
                    mm_multi(items, kKd + [f"vtm.{j}" for j in range(4)], [kkv])
                    kvs.append((kv, kkv))
                for cidx in range(8):
                    kv, kkv = kvs[cidx // 4]
                    cq = cidx % 4
                    stt(Sch_[:, cidx + 1, :], Sch_[:, cidx, :], sz[:, cidx:cidx + 1], kv[:, cq * 128:(cq + 1) * 128],
                        ALU.mult, ALU.add, kSch + [ksz, kkv], kSch)
                cp(Sb16_[:], Sch_[:, 0:8, :], kSch, kSb)
                cp(Scar[:, hd, :], Sch_[:, 8, :], kSch, [f"Scar.{hd}"])
                yield
                ob, kob = MS()
                items = []
                for j in range(4):
                    items.append((ob[:, j * 128:(j + 1) * 128], vtm[:, j, hd * 128:(hd + 1) * 128], ms_[:, j * 128:(j + 1) * 128],
                                  True, False))
                    for hh in range(2):
                        cidx = 2 * j + hh
                        items.append((ob[:, cidx * 64:(cidx + 1) * 64], Sb16_[:, cidx, :], qi[:, cidx * 64:(cidx + 1) * 64],
                                      False, hh == 1))
                mm_multi(items, [kms, kqi] + kSb + [f"vtm.{j}" for j in range(4)], [kob])
                unpin(kms)
                unpin(kqi)
                sq, ksq = T16(pin=True)
                act(sq[:], ob[:], AF.Square, [kob], [ksq])
                t1, kt1 = T32(pin=True)
                act(t1[:, 0:512], ob[:], AF.Copy, [kob, "par"], [kt1], scale=pc("hgn"))
                yield "post"
                sb_, ksb = MM()
                mm(sb_[:], [(ones_b, sq[:])], ["cstb", ksq], [ksb])
                unpin(ksq)
                rs, krs = T32(pin=True)
                act(rs[:, 0:512], sb_[:], AF.Ln, [ksb, "dv"], [krs], scale=1.0 / 128, bias=eps_ap)
                yield
                act(rs[:, 0:512], rs[:, 0:512], AF.Exp, [krs], [krs], scale=-0.5)
                yield
                tt(t1[:, 0:512], t1[:, 0:512], rs[:, 0:512], ALU.mult, [kt1, krs], [kt1])
                unpin(krs)
                yield
                tt(mixo[:, 4 + hd, :], t1[:, 0:512], gsb[:, hd, :], ALU.mult, [kt1, f"gsb.{hd}"], [f"mixo.{4 + hd}.{j}" for j in range(4)])
                unpin(kt1)

            ag = [a_chain(c) for c in range(4)]
            bg = [b_chain(h) for h in range(4)]
            bpre = set()

            def bstep(n):
                for _ in range(n):
                    for g in bg[0:2]:
                        if g not in bpre and next(g) == "pre":
                            bpre.add(g)
            rp = rest_proj()
            for na, nb in ((1, 2), (1, 2), (2, 2), (3, 2), (1, 0), (1, 0), (1, 0), (1, 0), (0, 1), (0, 1), (0, 1), (0, 1)):
                next(rp, None)
                for _ in range(na):
                    for g in ag:
                        next(g, None)
                bstep(nb)
            run_rr([rp] + ag)
            while len(bpre) < 2:
                bstep(1)
            Sch1 = split_last(actf[:, 0:5, :].rearrange("p a b -> p (a b)")[:, 0:1152], 9, 128)
            Sb1 = split_last(actb[:, 5:7, :].rearrange("p a b -> p (a b)"), 8, 128)
            kdlo1 = split_last(actb[:, 7, :], 4, 128)
            kdhi1 = split_last(actb[:, 16, :], 4, 128)
            kS1, kB1, kK1 = [f"act.{k}" for k in range(5)], ["act.5", "act.6"], ["act.7", "act.16"]
            p.op(DVE, lambda e: e.memset(kdlo1[64:128], 0.0), writes=["act.7"])
            p.op(DVE, lambda e: e.memset(kdhi1[0:64], 0.0), writes=["act.16"])
            bsets = [(kdTlo, kdThi, Sch, Sb16, ["kdT"], ["Sch"], ["Sb16"]),
                     (kdlo1, kdhi1, Sch1, Sb1, kK1, kS1, kB1)]
            def drive(targets):
                active = dict(targets)
                while active:
                    for g in list(active):
                        v = next(g, "end")
                        if v == "end" or (active[g] is not None and v == active[g]):
                            del active[g]
            drive({bg[0]: "post", bg[1]: "post", bg[2]: "pre", bg[3]: "pre"})
            drive({bg[2]: "post", bg[3]: "post", bg[0]: None, bg[1]: None})
            run_rr(bg)

        def odd_mixer():
            for i in range(2):
                (s,) = w_acquire(1)
                for cc in range(4):
                    g = i * 4 + cc
                    bank, kb = fm_mm(s, 8, cc)
                    act(qfb[:, cc, :] if i == 0 else gsb[:, cc, :], bank[:], AF.Gelu_apprx_tanh, [kb, "par"],
                        [f"qfb.{cc}" if i == 0 else f"gsb.{cc}"], bias=pc("odbu", g))
                w_release(1)
            slots = w_acquire(2)
            vs = {}
            st_, kst = ost, "ost"
            pend = None
            for j in range(4):
                for n in range(2):
                    bank, kb = MM()
                    mm(bank[:], [(xnT[:, k, j * 128:(j + 1) * 128], ring[slots[n]][:, k, :]) for k in range(8)],
                       [f"ring{slots[n]}"] + XNT, [kb])
                    vv, kvv = T32(pin=True)
                    tt(vv[:, 0:512], bank[:], odbv[:, n * 512:(n + 1) * 512], ALU.add, [kb, "odbv"], [kvv])
                    if pend is not None:
                        pend()
                    act(vv[:, 0:512], vv[:, 0:512], AF.Gelu_apprx_tanh, [kvv], [kvv, "ost.s"],
                        accum_out=st_[:, 2 * j + n:2 * j + n + 1])
                    vs[j, n] = (vv, kvv)

                    def sumsq(vv=vv, kvv=kvv, j=j, n=n):
                        jk, kjk = T16()
                        p.op(DVE, lambda e: e.scalar_tensor_tensor(
                            out=jk[:], in0=vv[:, 0:512], scalar=1.0, in1=vv[:, 0:512], op0=ALU.mult, op1=ALU.mult,
                            accum_out=st_[:, 8 + 2 * j + n:8 + 2 * j + n + 1]), [kvv], [kjk, "ost.q"])
                    pend = sumsq
            pend()
            w_release(2)
            p.op(DVE, lambda e: e.tensor_reduce(out=st_[:, 16:20], in_=split_last(st_[:, 0:8], 4, 2), axis=AX.X,
                                                op=ALU.add), ["ost.s"], [kst])
            p.op(DVE, lambda e: e.tensor_reduce(out=st_[:, 20:24], in_=split_last(st_[:, 8:16], 4, 2), axis=AX.X,
                                                op=ALU.add), ["ost.q"], [kst])
            ts(st_[:, 16:20], st_[:, 16:20], 1.0 / 1024, None, ALU.mult, None, [kst], [kst])
            tt(st_[:, 24:28], st_[:, 16:20], st_[:, 16:20], ALU.mult, [kst], [kst])
            stt(st_[:, 20:24], st_[:, 20:24], 1.0 / 1024, st_[:, 24:28], ALU.mult, ALU.subtract, [kst], [kst])
            act(st_[:, 20:24], st_[:, 20:24], AF.Ln, [kst, "dv"], [kst], bias=eps_ap)
            act(st_[:, 20:24], st_[:, 20:24], AF.Exp, [kst], [kst], scale=-0.5)
            stt(st_[:, 24:28], st_[:, 16:20], -1.0, st_[:, 20:24], ALU.mult, ALU.mult, [kst], [kst])
            wslots = w_acquire(2)
            ns4 = norm_subs("gffn1")

            def wout(j):
                adds = []
                for n in range(2):
                    bank, kb = MM()
                    mm(bank[:], [(mixo[:, k, j * 128:(j + 1) * 128], ring[wslots[n]][:, k, :]) for k in range(8)],
                       [f"ring{wslots[n]}"] + [f"mixo.{k}.{j}" for k in range(8)], [kb])
                    adds.append((bank, kb, n))
                for bank, kb, n in adds:
                    tt(H[:, j, n * 512:(n + 1) * 512], bank[:], H[:, j, n * 512:(n + 1) * 512], ALU.add,
                       [kb, f"H.{j}"], [f"H.{j}"])
                steps(ns4[j], 3)

            for j in range(4):
                vhs = []
                for n in range(2):
                    vv, kvv = vs[j, n]
                    vh, kvh = T16(pin=True)
                    act(vh[:], vv[:, 0:512], AF.Identity, [kvv, kst], [kvh],
                        scale=st_[:, 20 + j:21 + j], bias=st_[:, 24 + j:25 + j])
                    unpin(kvv)
                    vhs.append((vh, kvh))
                for gq in range(2):
                    vh, kvh = vhs[gq]
                    sv, ksv = MS()
                    mm_multi([(sv[:, gg * 128:(gg + 1) * 128], vh[:, gg * 128:(gg + 1) * 128],
                               WcT[:, gq * 4 + gg, :], True, True) for gg in range(4)], [kvh, "WcT"], [ksv])
                    unpin(kvh)
                    ubuf, un = (qfb, "qfb") if gq == 0 else (gsb, "gsb")
                    tq, ktq = T32()
                    tq3 = split_last(tq[:, 0:512], 4, 128)
                    tt(tq3, split_last(sv[:], 4, 128), bcast_last(pc("odg", 4 * gq, 4), 128), ALU.mult,
                       [ksv, "par"], [ktq])
                    tt(tq3, tq3, Qg[:, 4 * gq:4 * gq + 4, :], ALU.add, [ktq, "Qg"], [ktq])
                    tt(mixo[:, 4 * gq:4 * gq + 4, j * 128:(j + 1) * 128], tq3, ubuf[:, 0:4, j * 128:(j + 1) * 128],
                       ALU.mult, [ktq] + [f"{un}.{c}" for c in range(4)],
                       [f"mixo.{4 * gq + c}.{j}" for c in range(4)], eng=POOL)
                if j >= 1:
                    wout(j - 1)
                if j >= 2:
                    steps(ns4[j - 2], 2)
            wout(3)
            w_release(2)
            steps(ns4[2], 2)
            steps(ns4[3], 2)
            assert all(next(g, "end") == "end" for g in ns4)

        def final_norm(ti):
            r0 = ti * TT
            ss, ks = SM()
            for j in range(4):
                act(junk[:], H[:, j, :], AF.Square, [f"H.{j}"], [ks], accum_out=ss[:, j:j + 1])
            act(ss[:, 4:8], ss[:, 0:4], AF.Ln, [ks, "dv"], [ks], scale=1.0 / D, bias=eps_ap)
            act(ss[:, 4:8], ss[:, 4:8], AF.Exp, [ks], [ks], scale=-0.5)
            for j in range(4):
                for n in range(2):
                    o_, ko = T32()
                    stt(o_[:, 0:512], H[:, j, n * 512:(n + 1) * 512], ss[:, 4 + j:5 + j], gfin[:, n * 512:(n + 1) * 512],
                        ALU.mult, ALU.mult, [f"H.{j}", ks, "gfin"], [ko])
                    p.dma(SP, f"os{2 * j + n}", out[r0 + j * 128:r0 + (j + 1) * 128, n * 512:(n + 1) * 512],
                          o_[:, 0:512], reads=[ko])

        prefetch_x(0)
        run_rr([norm_gen("gmix0", stage)])
        for ti in range(NT):
            stage_to_H()
            last = ti + 1 >= NT
            stages = [even_mixer, proj_then_norm,
                      lambda: None, lambda: ffn(0, ride=norm_subs("gmix1")), lambda: None, odd_mixer]
            for f in stages[:upto]:
                f()
            if not last:
                prefetch_x(ti + 1)
            ffn(1, [] if last else [norm_gen("gmix0", stage)])
            final_norm(ti)
        p.wait_all(SP, [f"os{j}" for j in range(8)])
        p.emit()
    return nc


def make_in_maps(inp, T=SEQ, nb=NB):
    cst, cstb = make_consts()
    par = pack_params(inp)
    f = lambda a: np.ascontiguousarray(np.asarray(a, np.float32))
    shared = {
        "par": par, "cst": cst, "cstb": cstb,
        "gfin": f(inp["norm_final"]).reshape(1, D),
        "odbv": f(inp["od_b_in"][0][1024:]).reshape(1, D),
        "odlb": f(inp["od_ln_b"][0]).reshape(1, D),
        "odbs": f(inp["od_b_s"][0]).reshape(1, 1024),
        "wsT": np.ascontiguousarray(f(inp["od_w_s"][0]).transpose(2, 0, 1).reshape(128, 1024)),
        "ga": f(inp["ev_gate_a_w"][0]), "gx": f(inp["ev_gate_x_w"][0]),
        "w_evin": f(inp["ev_w_in"][0]), "w_evout": f(inp["ev_w_out"][0]),
        "w_odin": f(inp["od_w_in"][0]), "w_odout": f(inp["od_w_out"][0]),
        "w_up0": f(inp["ffn_w_up"][0]), "w_up1": f(inp["ffn_w_up"][1]),
        "w_dn0": f(inp["ffn_w_down"][0]), "w_dn1": f(inp["ffn_w_down"][1]),
    }
    xs = np.asarray(inp["x"], np.float32)
    return [dict(shared, x=np.ascontiguousarray(xs[b, :T])) for b in range(nb)]


def kernel(**inputs):
    nc = build(SEQ)
    in_maps = make_in_maps(inputs, SEQ, NB)
    res = run_bass_kernel_spmd(nc, in_maps, core_ids=list(range(NB)))
    return np.stack([np.asarray(r["out"], np.float32) for r in res.results], axis=0)
```
